# Optimizing a Trainium2 kernel written in Bass

```python
import math
import jax, jax.numpy as jnp
from jax import lax
import numpy as np

D_MODEL = 1024
BATCH = 4
SEQ = 8192
DEPTH = 1

CHUNK = 64
D_MIX = D_MODEL
GDN_HEADS = 4
GDN_HEAD_DIM = 128
GDN_WIDTH = GDN_HEADS * GDN_HEAD_DIM
SC_WIDTH = D_MIX - GDN_WIDTH
SC_GROUPS = 8
GDN_CONV = 4
SC_CONV = 3
D_FF = -(-8 * D_MODEL // (3 * 256)) * 256
RMS_EPS = 1e-6
L2_EPS = 1e-6

Q_OFF = 0
K_OFF = GDN_WIDTH
V_OFF = 2 * GDN_WIDTH
Z_OFF = 3 * GDN_WIDTH
A_OFF = 4 * GDN_WIDTH
BETA_OFF = A_OFF + GDN_HEADS
SCB_OFF = BETA_OFF + GDN_HEADS
SCC_OFF = SCB_OFF + SC_WIDTH
SCH_OFF = SCC_OFF + SC_WIDTH
D_IN = SCH_OFF + SC_WIDTH

kernel_name = "hybrid_gdn_shortconv_adaln_block"


def rms_norm(x, w):
    x32 = x.astype(jnp.float32)
    y = x32 * lax.rsqrt(jnp.mean(x32 * x32, axis=-1, keepdims=True) + RMS_EPS)
    return (y * w.astype(jnp.float32)).astype(x.dtype)


def l2_normalize(x):
    x32 = x.astype(jnp.float32)
    return x32 * lax.rsqrt(jnp.sum(x32 * x32, axis=-1, keepdims=True) + L2_EPS)


def causal_depthwise_conv(x, w):
    k_width = w.shape[0]
    s = x.shape[1]
    xp = jnp.pad(x, ((0, 0), (k_width - 1, 0), (0, 0)))
    out = xp[:, 0:s] * w[0]
    for j in range(1, k_width):
        out = out + xp[:, j:j + s] * w[j]
    return out


def gated_delta_rule_chunked(q, k, v, g, beta):
    bsz, s, h, dk = q.shape
    dv = v.shape[-1]
    n = s // CHUNK
    f32 = jnp.float32

    def to_chunks(t):
        return t.astype(f32).reshape(bsz, n, CHUNK, h, -1).transpose(0, 3, 1, 2, 4)

    q = to_chunks(q) * (dk ** -0.5)
    k = to_chunks(k)
    v = to_chunks(v)
    g = g.astype(f32).reshape(bsz, n, CHUNK, h).transpose(0, 3, 1, 2)
    beta = beta.astype(f32).reshape(bsz, n, CHUNK, h).transpose(0, 3, 1, 2)
    g = jnp.cumsum(g, axis=-1)

    k_beta = k * beta[..., None]
    v_beta = v * beta[..., None]
    tril = jnp.tril(jnp.ones((CHUNK, CHUNK), dtype=bool))
    strict = jnp.tril(jnp.ones((CHUNK, CHUNK), dtype=bool), -1)
    decay = jnp.exp(jnp.where(tril, g[..., :, None] - g[..., None, :], -jnp.inf))

    lmat = jnp.where(strict, jnp.einsum('bhncd,bhnmd->bhncm', k_beta, k) * decay, 0.0)
    eye = jnp.broadcast_to(jnp.eye(CHUNK, dtype=f32), lmat.shape)
    tmat = lax.linalg.triangular_solve(lmat, eye, left_side=True, lower=True, unit_diagonal=True)

    u = jnp.einsum('bhncm,bhnme->bhnce', tmat, v_beta)
    w = jnp.einsum('bhncm,bhnmd->bhncd', tmat, k_beta * jnp.exp(g)[..., None])
    attn = jnp.where(tril, jnp.einsum('bhncd,bhnmd->bhncm', q, k) * decay, 0.0)
    g_last = g[..., -1]
    k_dec = k * jnp.exp(g_last[..., None] - g)[..., None]
    q_dec = q * jnp.exp(g)[..., None]

    def step(state, inp):
        qd, wi, ui, ai, kd, gl = inp
        v_new = ui - jnp.einsum('bhcd,bhde->bhce', wi, state)
        out = jnp.einsum('bhcd,bhde->bhce', qd, state) + jnp.einsum('bhcm,bhme->bhce', ai, v_new)
        state = state * jnp.exp(gl)[..., None, None] + jnp.einsum('bhcd,bhce->bhde', kd, v_new)
        return state, out

    xs = tuple(jnp.moveaxis(t, 2, 0) for t in (q_dec, w, u, attn, k_dec, g_last))
    state0 = jnp.zeros((bsz, h, dk, dv), dtype=f32)
    _, outs = lax.scan(step, state0, xs)
    return outs.transpose(1, 0, 3, 2, 4).reshape(bsz, s, h, dv)


def hybrid_mixer(h, w_in, conv_qkv_w, a_log, dt_bias, gdn_norm_w, conv_short_w, w_out):
    bsz, s, _ = h.shape
    p = h @ w_in
    qkv = jax.nn.silu(causal_depthwise_conv(p[..., Q_OFF:Z_OFF], conv_qkv_w))
    q = qkv[..., 0:GDN_WIDTH].reshape(bsz, s, GDN_HEADS, GDN_HEAD_DIM)
    k = qkv[..., GDN_WIDTH:2 * GDN_WIDTH].reshape(bsz, s, GDN_HEADS, GDN_HEAD_DIM)
    v = qkv[..., 2 * GDN_WIDTH:3 * GDN_WIDTH].reshape(bsz, s, GDN_HEADS, GDN_HEAD_DIM)
    z = p[..., Z_OFF:A_OFF].reshape(bsz, s, GDN_HEADS, GDN_HEAD_DIM)
    a = p[..., A_OFF:BETA_OFF].astype(jnp.float32)
    b = p[..., BETA_OFF:SCB_OFF].astype(jnp.float32)
    g = -jnp.exp(a_log.astype(jnp.float32)) * jax.nn.softplus(a + dt_bias.astype(jnp.float32))
    beta = jax.nn.sigmoid(b)
    o = gated_delta_rule_chunked(l2_normalize(q), l2_normalize(k), v, g, beta)
    o = rms_norm(o, gdn_norm_w) * jax.nn.silu(z.astype(jnp.float32))
    y_gdn = o.reshape(bsz, s, GDN_WIDTH).astype(h.dtype)
    sc_b = p[..., SCB_OFF:SCC_OFF]
    sc_c = p[..., SCC_OFF:SCH_OFF]
    sc_h = p[..., SCH_OFF:D_IN]
    y_sc = sc_b * causal_depthwise_conv(sc_c * sc_h, conv_short_w)
    return jnp.concatenate([y_gdn, y_sc], axis=-1) @ w_out


def swiglu(h, w_gate, w_up, w_down):
    return (jax.nn.silu(h @ w_gate) * (h @ w_up)) @ w_down


def setup_inputs(seed: int = 0) -> dict:
    key = jax.random.key(seed)
    ks = jax.random.split(key, 20)
    f32 = jnp.float32
    nrm = lambda k, shape, scale: jax.random.normal(k, shape, f32) * scale
    dt = jnp.exp(jax.random.uniform(ks[8], (DEPTH, GDN_HEADS), f32, math.log(1e-3), math.log(1e-1)))
    return {
        "x": nrm(ks[0], (BATCH, SEQ, D_MODEL), 1.0),
        "c": nrm(ks[1], (BATCH, D_MODEL), 1.0),
        "w_ada": nrm(ks[2], (DEPTH, D_MODEL, 6 * D_MODEL), 0.5 * D_MODEL ** -0.5),
        "b_ada": nrm(ks[3], (DEPTH, 6 * D_MODEL), 0.02),
        "norm1_w": 1.0 + nrm(ks[4], (DEPTH, D_MODEL), 0.02),
        "w_in": nrm(ks[5], (DEPTH, D_MODEL, D_IN), D_MODEL ** -0.5),
        "conv_qkv_w": nrm(ks[6], (DEPTH, GDN_CONV, 3 * GDN_WIDTH), GDN_CONV ** -0.5),
        "A_log": jnp.log(jax.random.uniform(ks[7], (DEPTH, GDN_HEADS), f32, 1.0, 16.0)),
        "dt_bias": dt + jnp.log(-jnp.expm1(-dt)),
        "gdn_norm_w": 1.0 + nrm(ks[9], (DEPTH, GDN_HEAD_DIM), 0.02),
        "conv_short_w": nrm(ks[10], (DEPTH, SC_CONV, SC_WIDTH), SC_CONV ** -0.5),
        "w_out": nrm(ks[11], (DEPTH, D_MIX, D_MODEL), D_MIX ** -0.5),
        "norm2_w": 1.0 + nrm(ks[12], (DEPTH, D_MODEL), 0.02),
        "w_gate": nrm(ks[13], (DEPTH, D_MODEL, D_FF), D_MODEL ** -0.5),
        "w_up": nrm(ks[14], (DEPTH, D_MODEL, D_FF), D_MODEL ** -0.5),
        "w_down": nrm(ks[15], (DEPTH, D_FF, D_MODEL), D_FF ** -0.5),
        "norm_f_w": 1.0 + nrm(ks[16], (D_MODEL,), 0.02),
    }


def reference(x, c, w_ada, b_ada, norm1_w, w_in, conv_qkv_w, A_log, dt_bias, gdn_norm_w,
              conv_short_w, w_out, norm2_w, w_gate, w_up, w_down, norm_f_w):
    c_act = jax.nn.silu(c)
    for l in range(DEPTH):
        mod = (c_act @ w_ada[l] + b_ada[l])[:, None, :]
        shift1, scale1, gate1, shift2, scale2, gate2 = jnp.split(mod, 6, axis=-1)
        h = rms_norm(x, norm1_w[l]) * (1.0 + scale1) + shift1
        x = x + gate1 * hybrid_mixer(h, w_in[l], conv_qkv_w[l], A_log[l], dt_bias[l],
                                     gdn_norm_w[l], conv_short_w[l], w_out[l])
        h = rms_norm(x, norm2_w[l]) * (1.0 + scale2) + shift2
        x = x + gate2 * swiglu(h, w_gate[l], w_up[l], w_down[l])
    return rms_norm(x, norm_f_w)
```

```python
import numpy as np
from contextlib import ExitStack
import concourse.bass as bass
import concourse.mybir as mybir
from concourse.bass_utils import run_bass_kernel_spmd

F32 = mybir.dt.float32
BF16 = mybir.dt.bfloat16
AF = mybir.ActivationFunctionType
ALU = mybir.AluOpType

TT = 512
NT = 8
NEG = -30000.0
RMS_EPS = 1e-6
L2_EPS = 1e-6

V_C, V_BADA, V_N1, V_N2, V_NF, V_CW, V_CS, V_ALOG, V_DTB, V_GNW, V_PM = 0, 8, 56, 64, 72, 80, 128, 140, 144, 148, 149
NV = 150


class Tk:
    __slots__ = ("sem", "val", "key")

    def __init__(self, sem, val, key):
        self.sem, self.val, self.key = sem, val, key


class Res:
    __slots__ = ("w", "r")

    def __init__(self):
        self.w = None
        self.r = {}


class Eng:
    def __init__(self, name, sem):
        self.name, self.sem = name, sem
        self.count = 0
        self.waited = {}
        self.prog = []


class DSem:
    def __init__(self, sem, key):
        self.sem, self.key, self.count = sem, key, 0


class FW:
    def __init__(self, nc, stack):
        self.nc, self.stack = nc, stack
        mk = lambda n: Eng(n, stack.enter_context(nc.semaphore("s_" + n)))
        self.PE, self.ACT, self.DVE, self.POOL, self.SP = mk("pe"), mk("act"), mk("dve"), mk("pool"), mk("sp")
        self.nds = 0

    def dsem(self):
        self.nds += 1
        return DSem(self.stack.enter_context(self.nc.semaphore("s_d%d" % self.nds)), "d%d" % self.nds)

    def sb(self, name, shape, dt):
        return self.stack.enter_context(self.nc.sbuf_tensor(name, list(shape), dt))

    def ps(self, name, shape, dt):
        return self.stack.enter_context(self.nc.psum_tensor(name, list(shape), dt))

    def _wait(self, E, t):
        if t.key == E.name and E.name == "pe":
            return
        if E.waited.get(t.key, 0) >= t.val:
            return
        E.prog.append(("wait", t.sem, t.val))
        E.waited[t.key] = t.val

    @staticmethod
    def _flat(lst):
        out = []
        for b in lst:
            if isinstance(b, (list, tuple)):
                out.extend(FW._flat(b))
            elif b is not None:
                out.append(b)
        return out

    def _deps(self, E, reads, writes):
        reads, writes = self._flat(reads), self._flat(writes)
        for b in reads:
            if b.w is not None:
                self._wait(E, b.w)
        for b in writes:
            if b.w is not None and b.w.key != E.name:
                self._wait(E, b.w)
            for t in b.r.values():
                if t.key != E.name:
                    self._wait(E, t)

    def _mark(self, tk, reads, writes):
        reads, writes = self._flat(reads), self._flat(writes)
        for b in reads:
            o = b.r.get(tk.key)
            if o is None or o.val < tk.val:
                b.r[tk.key] = tk
        for b in writes:
            b.w = tk
            b.r = {}

    def op(self, E, fn, reads=(), writes=()):
        self._deps(E, reads, writes)
        E.count += 1
        tk = Tk(E.sem, E.count, E.name)
        E.prog.append(("op", fn))
        self._mark(tk, reads, writes)
        return tk

    def dma(self, E, ds, fn, reads=(), writes=()):
        self._deps(E, reads, writes)
        ds.count += 16
        tk = Tk(ds.sem, ds.count, ds.key)
        E.prog.append(("dma", fn, ds.sem))
        self._mark(tk, reads, writes)
        return tk

    def finish(self):
        engs = [(self.PE, "tensor"), (self.ACT, "scalar"), (self.DVE, "vector"), (self.POOL, "gpsimd"), (self.SP, "sync")]
        with self.nc.Block() as block:
            for E, attr in engs:
                def body(eng, E=E):
                    for it in E.prog:
                        if it[0] == "wait":
                            eng.wait_ge(it[1], it[2])
                        elif it[0] == "op":
                            it[1](eng).then_inc(E.sem, 1)
                        else:
                            it[1](eng).then_inc(it[2], 16)
                getattr(block, attr)(body)


class B:
    def __init__(self, t, nres=1):
        self.t = t
        if nres == 1:
            self.r = Res()
        else:
            self.rh = [Res() for _ in range(nres)]
            self.r = tuple(self.rh)

    def __getitem__(self, k):
        return self.t[k]


DBG = {"on": False, "seq": None, "names": []}


def build_program():
    nc = bass.Bass("TRN2", target_bir_lowering=False)
    D = {}
    DBG["names"] = []

    def din(name, shape):
        D[name] = nc.dram_tensor(name, list(shape), F32, kind="ExternalInput").ap()

    din("xm", [1024, 4096]); din("xp", [1024, 4096])
    din("consts", [128, 7 * 128]); din("vecs", [128, NV])
    din("wada", [12 * 128, 4096]); din("win", [7 * 128, 4096]); din("wab", [128, 64])
    din("wout", [2 * 128, 4096]); din("wg", [6 * 128, 4096]); din("wu", [6 * 128, 4096]); din("wd", [8 * 128, 2816])
    outd = nc.dram_tensor("out", [1024, 4096], F32, kind="ExternalOutput").ap()

    st = ExitStack()
    with st:
        fw = FW(nc, st)
        PE, ACT, DVE, POOL, SP = fw.PE, fw.ACT, fw.DVE, fw.POOL, fw.SP

        def sb(name, shape, dt=F32, nres=1):
            return B(fw.sb("sb_" + name, shape, dt), nres)

        def mm(out, lhsT, rhs, start, stop, rd, wr):
            fw.op(PE, lambda e: e.matmul(out, lhsT=lhsT, rhs=rhs, start=start, stop=stop), rd, wr)

        def tr(out, in_, ident, rd, wr):
            fw.op(PE, lambda e: e.transpose(out, in_, ident), rd, wr)

        def act(out, in_, func, rd, wr, bias=None, scale=None):
            kw = {}
            if bias is not None:
                kw["bias"] = bias
            if scale is not None:
                kw["scale"] = scale
            fw.op(ACT, lambda e: e.activation(out=out, in_=in_, func=func, **kw), rd, wr)

        def tt(out, in0, in1, op, rd, wr, E=None):
            fw.op(E or DVE, lambda e: e.tensor_tensor(out=out, in0=in0, in1=in1, op=op), rd, wr)

        def ts(out, in0, s1, op0, rd, wr, s2=None, op1=None, E=None):
            if op1 is None:
                fw.op(E or DVE, lambda e: e.tensor_scalar(out=out, in0=in0, scalar1=s1, scalar2=None, op0=op0), rd, wr)
            else:
                fw.op(E or DVE, lambda e: e.tensor_scalar(out=out, in0=in0, scalar1=s1, scalar2=s2, op0=op0, op1=op1), rd, wr)

        def stt(out, in0, scalar, in1, op0, op1, rd, wr):
            fw.op(DVE, lambda e: e.scalar_tensor_tensor(out=out, in0=in0, scalar=scalar, in1=in1, op0=op0, op1=op1), rd, wr)

        def cp(out, in_, rd, wr, E=None):
            fw.op(E or DVE, lambda e: e.tensor_copy(out=out, in_=in_), rd, wr)

        def recip(out, in_, rd, wr):
            fw.op(DVE, lambda e: e.reciprocal(out=out, in_=in_), rd, wr)

        def h4(ap):
            return ap.rearrange("p (h c) -> p h c", h=4)

        consts = sb("consts", [128, 7 * 128])
        idf = consts.t[:, 0:128]; tri = consts.t[:, 128:256]; blk = consts.t[:, 256:384]
        indA = consts.t[:, 384:512]; indB = consts.t[:, 512:640]; mnegs = consts.t[:, 640:768]; mnegi = consts.t[:, 768:896]
        vecs = sb("vecs", [128, NV])
        idb = sb("idb", [128, 128], BF16)
        ones = sb("ones", [128, 128], BF16)
        mod = sb("mod", [128, 64])
        cact = sb("cact", [128, 8], BF16)
        ctmp = sb("ctmp", [128, 16])
        negA = sb("negA", [128, 4])
        wab = sb("wab", [128, 64], BF16)
        xs = [sb("xs%d" % i, [128, 8, TT]) for i in range(2)]
        xds = [fw.dsem() for _ in range(2)]
        hT = sb("hT", [128, 8, TT], BF16)
        h2T = sb("h2T", [128, 8, TT], BF16)
        yT = sb("yT", [128, 8, TT], BF16, nres=8)
        sqb = [sb("sqb%d" % i, [128, TT], BF16) for i in range(2)]
        tmp = [sb("tmp%d" % i, [128, TT]) for i in range(1)]
        NSLOT = 6
        wr_ = [sb("wr%d" % i, [128, 2048], BF16) for i in range(NSLOT)]
        for w_ in wr_:
            w_.pinned = False
        wds = [fw.dsem() for _ in range(NSLOT)]
        qn = sb("qn", [128, 4, TT], BF16); kn = sb("kn", [128, 4, TT], BF16); vT = sb("vT", [128, 4, TT], BF16)
        zs = sb("zs", [128, 4, TT], BF16, nres=4)
        raw = [sb("raw%d" % i, [128, TT]) for i in range(4)]
        pin = [sb("cx%d" % i, [128, TT + 3]) for i in range(3)]
        acc = pin
        halo = sb("halo", [128, 16, 3])
        sccs = sb("sccs", [128, 4, TT], BF16, nres=4); cvo = sb("cvo", [128, 4, TT], BF16, nres=4)
        sgb = [sb("sgb%d" % i, [128, TT]) for i in range(2)]
        arena = fw.sb("sb_arena", [128, 6144], F32)
        pages = [Res() for _ in range(24)]

        class V:
            def __init__(self, t, r):
                self.t, self.r = t, r
        arena_bf = arena[:].bitcast(BF16)
        Ot = sb("Ot", [128, 4, TT], BF16, nres=4)
        S = sb("S", [128, 4, 128], nres=4); Sb = sb("Sb", [128, 4, 128], BF16)
        actv = ([V(yT.t[:, j, :], yT.rh[j]) for j in range(8)] + [V(sccs.t[:, j, :], sccs.rh[j]) for j in range(4)]
                + [V(cvo.t[:, j, :], cvo.rh[j]) for j in range(4)] + [V(zs.t[:, j, :], zs.rh[j]) for j in range(4)]
                + [V(Ot.t[:, j, :], Ot.rh[j]) for j in range(2)])
        gt = {n: sb("g_" + n, [128, 16]) for n in ["xa", "ax", "e1", "l1", "g", "ng", "eb", "opb", "beta", "lnopb", "cP", "cA", "skbg", "skd", "t1"]}
        gsb = sb("gsb", [128, 64]); eGlS = [sb("eGl%d" % i, [128, 32]) for i in range(2)]; absb = sb("absb", [128, 32])
        nrepS = [sb("nrep%d" % i, [128, 4, 128], nres=4) for i in range(2)]
        EsS = [sb("Es%d" % i, [128, 4, 128], nres=4) for i in range(2)]
        PmS = [V(arena[:, c * 1536:c * 1536 + 512].rearrange("p (h c) -> p h c", h=4), tuple(pages[6 * c:6 * c + 2])) for c in range(4)]
        PRS = [V(arena[:, c * 1536 + 512:(c + 1) * 1536].rearrange("p (h t c) -> p h t c", h=4, t=2), tuple(pages[6 * c + 2:6 * c + 6])) for c in range(4)]
        for pr_ in PRS:
            pr_.ht = {(hp, t_): Res() for hp in range(2) for t_ in range(2)}
        TTbS = [sb("TTb%d" % i, [128, 4, 128], BF16) for i in range(4)]
        KbgS = [sb("Kbg%d" % i, [128, 4, 128], BF16) for i in range(4)]
        KdS = [sb("Kd%d" % i, [128, 4, 128], BF16) for i in range(4)]
        VbS = [sb("Vb%d" % i, [128, 4, 128], BF16) for i in range(4)]
        attnTS = [sb("attnT%d" % i, [128, 4, 128], BF16) for i in range(4)]
        QdS = [sb("Qd%d" % i, [128, 4, 128], BF16) for i in range(4)]
        nWtS = [sb("nWt%d" % i, [128, 4, 128], BF16) for i in range(4)]
        Vn = sb("Vn", [128, 4, 128], BF16)
        banks = [B(fw.ps("pb%d" % i, [128, 512], F32)) for i in range(8)]
        bstate = {"i": 0, "w": 0, "x": 0, "s": 0, "t": 0, "pin": 0, "acc": 0, "sg": 0, "cx": 0}

        def nb():
            for _k in range(8):
                b = banks[bstate["i"] % 8]
                bstate["i"] += 1
                if not getattr(b, "pinned", False):
                    return b
            raise RuntimeError("all psum banks pinned")

        def rot(lst, key):
            b = lst[bstate[key] % len(lst)]
            bstate[key] += 1
            return b

        misc_ds = fw.dsem()
        out_ds = fw.dsem()
        dbg_ds = fw.dsem()
        cur = {"n": -1}

        def dbg(tag, ap, res, cond=True):
            if not (DBG["on"] and cond):
                return
            name = "dbg_%s_%d" % (tag, cur["n"])
            if name in DBG["names"]:
                return
            DBG["names"].append(name)
            shp = list(ap.shape)
            dt_ = ap.dtype
            dr = nc.dram_tensor(name, shp, dt_, kind="ExternalOutput").ap()
            fw.dma(SP, dbg_ds, lambda e: e.dma_start(out=dr, in_=ap), [res], [])

        def piece(src_ap, nj, n):
            for k in range(NSLOT):
                i = (bstate["w"] + k) % NSLOT
                if not wr_[i].pinned:
                    break
            else:
                raise RuntimeError("all weight slots pinned")
            bstate["w"] = i + 1
            slot = wr_[i]
            slot.pinned = True
            dst = slot.t[:, 0:nj * n].rearrange("p (j n) -> p j n", j=nj)
            fw.dma(POOL, wds[i], lambda e: e.dma_start(out=dst, in_=src_ap, max_dma_last_dim=4096), [], [slot.r])
            slot.wv = dst
            return slot

        def piece_rows(dname, row0, nj, ncols, c0, n):
            src = D[dname][row0:row0 + 128, 0:nj * ncols].rearrange("p (j n) -> p j n", j=nj)[:, :, c0:c0 + n]
            return piece(src, nj, n)

        fw.dma(SP, misc_ds, lambda e: e.dma_start(out=consts.t[:], in_=D["consts"]), [], [consts.r])
        misc_ds2 = fw.dsem()
        fw.dma(SP, misc_ds2, lambda e: e.dma_start(out=vecs.t[:], in_=D["vecs"]), [], [vecs.r])
        wabds = fw.dsem()
        fw.dma(POOL, wabds, lambda e: e.dma_start(out=wab.t[:], in_=D["wab"]), [], [wab.r])
        cp(idb.t[:], idf, [consts.r], [idb.r])
        fw.op(DVE, lambda e: e.memset(ones.t[:], 1.0), [], [ones.r])
        fw.op(DVE, lambda e: e.memset(halo.t[:], 0.0), [], [halo.r])
        fw.op(DVE, lambda e: e.memset(S.t[:], 0.0), [], [S.r])
        fw.op(DVE, lambda e: e.memset(Sb.t[:], 0.0), [], [Sb.r])
        act(ctmp.t[:, 0:8], vecs.t[:, V_C:V_C + 8], AF.Exp, [vecs.r], [ctmp.r], scale=-1.0)
        ts(ctmp.t[:, 0:8], ctmp.t[:, 0:8], 1.0, ALU.add, [ctmp.r], [ctmp.r])
        recip(ctmp.t[:, 0:8], ctmp.t[:, 0:8], [ctmp.r], [ctmp.r])
        tt(cact.t[:], ctmp.t[:, 0:8], vecs.t[:, V_C:V_C + 8], ALU.mult, [ctmp.r, vecs.r], [cact.r])
        act(negA.t[:], vecs.t[:, V_ALOG:V_ALOG + 4], AF.Exp, [vecs.r], [negA.r])
        ts(negA.t[:], negA.t[:], -1.0, ALU.mult, [negA.r], [negA.r])
        mb = nb()
        for pc in range(12):
            for hf_ in range(2):
                slot = piece_rows("wada", pc * 128, 8, 512, hf_ * 256, 256)
                wv = slot.wv
                for o2 in range(2):
                    col = pc * 4 + hf_ * 2 + o2
                    for j in range(8):
                        mm(mb.t[:, col:col + 1], wv[:, j, o2 * 128:(o2 + 1) * 128], cact.t[:, j:j + 1], j == 0, j == 7,
                           [slot.r, cact.r], [mb.r])
                slot.pinned = False
        tt(mod.t[:, 0:48], mb.t[:, 0:48], vecs.t[:, V_BADA:V_BADA + 48], ALU.add, [mb.r, vecs.r], [mod.r])
        dbg("vecs", vecs.t[:], vecs.r)
        stt(mod.t[:, 48:56], mod.t[:, 8:16], 1.0, vecs.t[:, V_N1:V_N1 + 8], ALU.add, ALU.mult, [mod.r, vecs.r], [mod.r])
        stt(mod.t[:, 56:64], mod.t[:, 32:40], 1.0, vecs.t[:, V_N2:V_N2 + 8], ALU.add, ALU.mult, [mod.r, vecs.r], [mod.r])
        shift1, gate1, shift2, gate2 = mod.t[:, 0:8], mod.t[:, 16:24], mod.t[:, 24:32], mod.t[:, 40:48]
        a1, a2 = mod.t[:, 48:56], mod.t[:, 56:64]

        def load_x(src, ti, slot_i):
            xb = xs[slot_i]
            srcv = src.rearrange("(j p) t -> p j t", p=128)[:, :, ti * TT:(ti + 1) * TT]
            fw.dma(SP, xds[slot_i], lambda e: e.dma_start(out=xb.t[:], in_=srcv), [], [xb.r])

        def rms_sumsq(xb, nj):
            bk = nb()
            for j, (ap, r) in enumerate(xb):
                sq = rot(sqb, "s")
                act(sq.t[:], ap, AF.Square, [r], [sq.r])
                mm(bk.t[:], ones.t[:], sq.t[:], j == 0, j == nj - 1, [ones.r, sq.r], [bk.r])
            return bk

        def rms_ln(bk, width_scale, eps):
            act(bk.t[:], bk.t[:], AF.Ln, [bk.r], [bk.r], bias=eps, scale=width_scale)

        def rms_exp(bk, extra_bias=None):
            if extra_bias is None:
                act(bk.t[:], bk.t[:], AF.Exp, [bk.r], [bk.r], scale=-0.5)
            else:
                act(bk.t[:], bk.t[:], AF.Exp, [bk.r], [bk.r], scale=-0.5, bias=extra_bias)

        def rms_rstd(xb, nj, width_scale, eps, extra_bias=None):
            bk = rms_sumsq(xb, nj)
            rms_ln(bk, width_scale, eps)
            rms_exp(bk, extra_bias)
            return bk

        def gen_norm(xb, a_, sh_, dst):
            rb = rms_rstd([(xb.t[:, j, :], xb.r) for j in range(8)], 8, 1.0 / 1024.0, RMS_EPS)
            rb.pinned = True
            yield
            for j in range(8):
                t_ = rot(tmp, "t")
                stt(t_.t[:], xb.t[:, j, :], a_[:, j:j + 1], rb.t[:], ALU.mult, ALU.mult, [xb.r, mod.r, rb.r], [t_.r])
                act(dst.t[:, j, :], t_.t[:], AF.Identity, [t_.r, mod.r], [dst.r], bias=sh_[:, j:j + 1])
                if j == 7:
                    rb.pinned = False
                yield

        def proj_chunk(wv, cc, slot, rhs_b, nk):
            bk = nb()
            for j in range(nk):
                mm(bk.t[:], wv[:, j, cc * 128:(cc + 1) * 128], rhs_b.t[:, j, :], j == 0, j == nk - 1, [slot.r, rhs_b.r], [bk.r])
            return bk

        def conv_chunk(src_ap, src_r, ch, ntap, wcol0, out_ap, out_r, from_psum_copy=True):
            p_ = rot(pin, "cx")
            hl = ntap - 1
            if from_psum_copy:
                act(p_.t[:, hl:hl + TT], src_ap, AF.Copy, src_r, [p_.r])
            else:
                src_ap(p_.t[:, hl:hl + TT], p_.r)
            cp(p_.t[:, 0:hl], halo.t[:, ch, 0:hl], [halo.r], [p_.r])
            cp(halo.t[:, ch, 0:hl], p_.t[:, TT:TT + hl], [p_.r], [halo.r])
            act(out_ap, p_.t[:, 0:TT], AF.Copy, [p_.r, vecs.r], [out_r], scale=vecs.t[:, wcol0:wcol0 + 1])
            for k in range(1, ntap):
                stt(out_ap, p_.t[:, k:k + TT], vecs.t[:, wcol0 + k:wcol0 + k + 1], out_ap, ALU.mult, ALU.add,
                    [p_.r, vecs.r, out_r], [out_r])

        def wpiece(i, hf_):
            slot = piece_rows("win", i * 128, 8, 512, hf_ * 256, 256)
            return slot, slot.wv

        def gen_proj_qkv(full):
            for pi in ([0, 1, 2] if full else [1, 2]):
                for cc in range(4):
                    if cc % 2 == 0:
                        slot, wv = wpiece(pi, cc // 2)
                    bk = proj_chunk(wv, cc % 2, slot, hT, 8)
                    if cc % 2 == 1:
                        slot.pinned = False
                    ch = pi * 4 + cc
                    a_ = rot(acc, "cx")
                    conv_chunk(bk.t[:], [bk.r], ch, 4, V_CW + ch * 4, a_.t[:, 0:TT], a_.r)
                    if pi == 2:
                        act(vT.t[:, cc, :], a_.t[:, 0:TT], AF.Silu, [a_.r], [vT.r])
                    else:
                        act(raw[cc].t[:], a_.t[:, 0:TT], AF.Silu, [a_.r], [raw[cc].r])
                    yield
                if pi != 2:
                    dst = qn if pi == 0 else kn
                    for c0 in (0, 2):
                        rbs = {}
                        for cc in (c0, c0 + 1):
                            rb = rms_sumsq([(raw[cc].t[:], raw[cc].r)], 1)
                            rb.pinned = True
                            rbs[cc] = rb
                        for cc in (c0, c0 + 1):
                            rms_ln(rbs[cc], 1.0, L2_EPS)
                        for cc in (c0, c0 + 1):
                            rms_exp(rbs[cc], (-0.5 * np.log(128.0) if pi == 0 else None))
                        yield
                        for cc in (c0, c0 + 1):
                            tt(dst.t[:, cc, :], raw[cc].t[:], rbs[cc].t[:], ALU.mult, [raw[cc].r, rbs[cc].r], [dst.r])
                            rbs[cc].pinned = False
                        yield
            abk = nb()
            wabv = wab.t[:].rearrange("p (j n) -> p j n", j=8)
            for ci in range(4):
                for j in range(8):
                    mm(abk.t[:, ci * 8:(ci + 1) * 8], hT.t[:, j, ci * 128:(ci + 1) * 128], wabv[:, j, :], j == 0, j == 7,
                       [hT.r, wab.r], [abk.r])
            cp(absb.t[:], abk.t[:, 0:32], [abk.r], [absb.r])
            yield

        def gen_proj_rest():
            for cc in range(4):
                if cc % 2 == 0:
                    slot, wv = wpiece(3, cc // 2)
                bk = proj_chunk(wv, cc % 2, slot, hT, 8)
                if cc % 2 == 1:
                    slot.pinned = False
                act(zs.t[:, cc, :], bk.t[:], AF.Silu, [bk.r], [zs.r])
                yield
            for cc in range(4):
                if cc % 2 == 0:
                    slot, wv = wpiece(4, cc // 2)
                bk = proj_chunk(wv, cc % 2, slot, hT, 8)
                if cc % 2 == 1:
                    slot.pinned = False
                act(sccs.t[:, cc, :], bk.t[:], AF.Copy, [bk.r], [sccs.r])
                yield
            for cc in range(4):
                if cc % 2 == 0:
                    slot, wv = wpiece(5, cc // 2)
                bk = proj_chunk(wv, cc % 2, slot, hT, 8)
                if cc % 2 == 1:
                    slot.pinned = False

                def prod(dst_ap, dst_r, bk=bk, cc=cc):
                    tt(dst_ap, bk.t[:], sccs.t[:, cc, :], ALU.mult, [bk.r, sccs.r], [dst_r])
                conv_chunk(prod, None, 12 + cc, 3, V_CS + cc * 3, cvo.t[:, cc, :], cvo.r, from_psum_copy=False)
                yield
            for cc in range(4):
                if cc % 2 == 0:
                    slot, wv = wpiece(6, cc // 2)
                bk = proj_chunk(wv, cc % 2, slot, hT, 8)
                if cc % 2 == 1:
                    slot.pinned = False
                tt(yT.t[:, 4 + cc, :], bk.t[:], cvo.t[:, cc, :], ALU.mult, [bk.r, cvo.r], [yT.r])
                yield

        def g16(n):
            return gt[n].t[:].rearrange("p (c h) -> p c h", c=4)

        def stage_gates(abk, masked, eGl):
            abv = abk.t[:].rearrange("p (c t h) -> p c t h", c=4, t=2)
            a_ap, b_ap = abv[:, :, 0, :], abv[:, :, 1, :]
            G = lambda n: gt[n]
            dtb = vecs.t[:, V_DTB:V_DTB + 4].unsqueeze(1).to_broadcast([128, 4, 4])
            nAb = negA.t[:].unsqueeze(1).to_broadcast([128, 4, 4])
            tt(g16("xa"), a_ap, dtb, ALU.add, [abk.r, vecs.r], [G("xa").r])
            act(G("ax").t[:], G("xa").t[:], AF.Abs, [G("xa").r], [G("ax").r])
            act(G("e1").t[:], G("ax").t[:], AF.Exp, [G("ax").r], [G("e1").r], scale=-1.0)
            act(G("l1").t[:], G("e1").t[:], AF.Ln, [G("e1").r], [G("l1").r], bias=1.0)
            stt(G("t1").t[:], G("xa").t[:], 0.0, G("l1").t[:], ALU.max, ALU.add, [G("xa").r, G("l1").r], [G("t1").r])
            tt(g16("g"), g16("t1"), nAb, ALU.mult, [G("t1").r, negA.r], [G("g").r])
            ts(G("ng").t[:], G("g").t[:], -1.0, ALU.mult, [G("g").r], [G("ng").r])
            act(g16("eb"), b_ap, AF.Exp, [abk.r], [G("eb").r], scale=-1.0)
            ts(G("opb").t[:], G("eb").t[:], 1.0, ALU.add, [G("eb").r], [G("opb").r])
            recip(G("beta").t[:], G("opb").t[:], [G("opb").r], [G("beta").r])
            act(G("lnopb").t[:], G("opb").t[:], AF.Ln, [G("opb").r], [G("lnopb").r])
            gb = nb()
            gsrc = G("g")
            mm(gb.t[:, 0:16], tri, gsrc.t[:], True, True, [consts.r, gsrc.r], [gb.r])
            mm(gb.t[:, 16:32], blk, gsrc.t[:], True, True, [consts.r, gsrc.r], [gb.r])
            mm(gb.t[:, 32:48], indA, gsrc.t[:], True, True, [consts.r, gsrc.r], [gb.r])
            mm(gb.t[:, 48:64], indB, gsrc.t[:], True, True, [consts.r, gsrc.r], [gb.r])
            cp(gsb.t[:], gb.t[:, 0:64], [gb.r], [gsb.r])
            Gc, Glo = gsb.t[:, 0:16], gsb.t[:, 16:32]
            tt(G("cP").t[:], Gc, G("lnopb").t[:], ALU.subtract, [gsb.r, G("lnopb").r], [G("cP").r])
            act(G("skbg").t[:], G("cP").t[:], AF.Exp, [G("cP").r], [G("skbg").r])
            ts(G("cA").t[:], Gc, -1.0, ALU.mult, [gsb.r], [G("cA").r])
            tt(G("skd").t[:], Glo, Gc, ALU.subtract, [gsb.r], [G("skd").r])
            act(G("skd").t[:], G("skd").t[:], AF.Exp, [G("skd").r], [G("skd").r])
            act(eGl.t[:], gsb.t[:, 32:64], AF.Exp, [gsb.r], [eGl.r])
            if masked:
                pm = vecs.t[:, V_PM:V_PM + 1]
                ts(G("skbg").t[:], G("skbg").t[:], pm, ALU.mult, [G("skbg").r, vecs.r], [G("skbg").r])
                ts(G("beta").t[:], G("beta").t[:], pm, ALU.mult, [G("beta").r, vecs.r], [G("beta").r])

        def bc4(ap_ci):
            return ap_ci.unsqueeze(2).to_broadcast([128, 4, 128])

        def bcm(mask_ap):
            return mask_ap.unsqueeze(1).to_broadcast([128, 4, 128])

        def gdn_front(ci, full):
            cs = slice(ci * 128, (ci + 1) * 128)
            G = lambda n: gt[n]
            nrep = nrepS[ci % 2]; Es = EsS[ci % 2]; Ea = nrep; eGb = Es
            Pm, PR = PmS[ci], PRS[ci]
            Kbg, Kd, Vb, attnT, Qd = KbgS[ci], KdS[ci], VbS[ci], attnTS[ci], QdS[ci]
            cp(nrep.t[:], bc4(g16("ng")[:, ci, :]), [G("ng").r], [nrep.r])
            bA = nb()
            for h in range(4):
                mm(bA.t[:, h * 128:(h + 1) * 128], nrep.t[:, h, :], tri, True, True, [nrep.rh[h], consts.r], [bA.r])
            bB = nb()
            for h in range(4):
                mm(bB.t[:, h * 128:(h + 1) * 128], kn.t[:, h, cs], kn.t[:, h, cs], True, True, [kn.r], [bB.r])
            tt(Es.t[:], h4(bA.t[:]), bcm(mnegs), ALU.add, [bA.r, consts.r], [Es.r])
            for h in range(4):
                act(Es.t[:, h, :], Es.t[:, h, :], AF.Exp, [Es.rh[h], G("cP").r], [Es.rh[h]], bias=g16("cP")[:, ci, h:h + 1])
            stt(Pm.t, h4(bB.t[:]), -1.0, Es.t[:], ALU.mult, ALU.mult, [bB.r, Es.r], [Pm.r])
            dbg("P0", Pm.t, Pm.r, ci == 0)
            bG = nb()
            bGb = bG.t[:].bitcast(BF16)
            for h in range(4):
                tr(bGb[:, h * 128:(h + 1) * 128], kn.t[:, h, cs], idb.t[:], [kn.r, idb.r], [bG.r])
            for h in range(4):
                tr(bGb[:, 512 + h * 128:512 + (h + 1) * 128], vT.t[:, h, cs], idb.t[:], [vT.r, idb.r], [bG.r])
            ktok = bGb[:, 0:512].rearrange("p (h c) -> p h c", h=4)
            vtok = bGb[:, 512:1024].rearrange("p (h c) -> p h c", h=4)
            tt(Kbg.t[:], ktok, bc4(g16("skbg")[:, ci, :]), ALU.mult, [bG.r, G("skbg").r], [Kbg.r])
            tt(Kd.t[:], ktok, bc4(g16("skd")[:, ci, :]), ALU.mult, [bG.r, G("skd").r], [Kd.r])
            tt(Vb.t[:], vtok, bc4(g16("beta")[:, ci, :]), ALU.mult, [bG.r, G("beta").r], [Vb.r])
            if full:
                bH = nb()
                for h in range(4):
                    mm(bH.t[:, h * 128:(h + 1) * 128], kn.t[:, h, cs], qn.t[:, h, cs], True, True, [kn.r, qn.r], [bH.r])
                tt(Ea.t[:], h4(bA.t[:]), bcm(mnegi), ALU.subtract, [bA.r, consts.r], [Ea.r])
                for h in range(4):
                    act(Ea.t[:, h, :], Ea.t[:, h, :], AF.Exp, [Ea.rh[h], G("cA").r], [Ea.rh[h]], bias=g16("cA")[:, ci, h:h + 1], scale=-1.0)
                tt(attnT.t[:], h4(bH.t[:]), Ea.t[:], ALU.mult, [bH.r, Ea.r], [attnT.r])
                act(eGb.t[:], h4(bA.t[:]), AF.Exp, [bA.r], [eGb.r], scale=-1.0)
                tt(Qd.t[:], qn.t[:, :, cs], eGb.t[:], ALU.mult, [qn.r, eGb.r], [Qd.r])
            bC = nb()
            for h in range(4):
                tr(bC.t[:, h * 128:(h + 1) * 128], Pm.t[:, h, :], idf, [Pm.r, consts.r], [bC.r])
            act(PR.t[:, :, 0, :], h4(bC.t[:]), AF.Copy, [bC.r], [PR.r, PR.ht[(0, 0)], PR.ht[(1, 0)]])
            tt(PR.t[:, :, 1, :], h4(bC.t[:]), bcm(idf), ALU.add, [bC.r, consts.r], [PR.r, PR.ht[(0, 1)], PR.ht[(1, 1)]])

        def gdn_level(ci, lvl):
            Pm, PR, TTb = PmS[ci], PRS[ci], TTbS[ci]
            if lvl == 0:
                bD = nb(); bE = nb()
                for h in range(4):
                    mm(bD.t[:, h * 128:(h + 1) * 128], PR.t[:, h, 0, :], Pm.t[:, h, :], True, True, [PR.ht[(h // 2, 0)], Pm.r], [bD.r])
                for h in range(4):
                    mm(bE.t[:, h * 128:(h + 1) * 128], Pm.t[:, h, :], PR.t[:, h, 0, :], True, True, [PR.ht[(h // 2, 0)], Pm.r], [bE.r])
                act(Pm.t, h4(bD.t[:]), AF.Copy, [bD.r], [Pm.r])
                cp(PR.t[:, :, 0, :], h4(bE.t[:]), [bE.r], [PR.ht[(0, 0)], PR.ht[(1, 0)]])
            elif lvl in (1, 2, 3):
                bD = nb(); bE = nb(); bF = nb()
                for h in range(4):
                    mm(bD.t[:, h * 128:(h + 1) * 128], PR.t[:, h, 0, :], Pm.t[:, h, :], True, True, [PR.ht[(h // 2, 0)], Pm.r], [bD.r])
                for h in range(4):
                    bb = bE if h < 2 else bF
                    o = (h % 2) * 256
                    mm(bb.t[:, o:o + 256], Pm.t[:, h, :], PR.t[:, h, :, :].rearrange("p t c -> p (t c)"), True, True,
                       [PR.ht[(h // 2, 0)], PR.ht[(h // 2, 1)], Pm.r], [bb.r])
                act(Pm.t, h4(bD.t[:]), AF.Copy, [bD.r], [Pm.r])
                for hp, (bb, hs) in enumerate(((bE, slice(0, 2)), (bF, slice(2, 4)))):
                    v = bb.t[:].rearrange("p (h t c) -> p h t c", h=2, t=2)
                    tt(PR.t[:, hs, 1, :], v[:, :, 1, :], PR.t[:, hs, 1, :], ALU.add, [bb.r, PR.ht[(hp, 1)]], [PR.ht[(hp, 1)]])
                    act(PR.t[:, hs, 0, :], v[:, :, 0, :], AF.Copy, [bb.r], [PR.ht[(hp, 0)]])
            elif lvl == 4:
                bD = nb(); bE = nb()
                for h in range(4):
                    mm(bD.t[:, h * 128:(h + 1) * 128], PR.t[:, h, 0, :], Pm.t[:, h, :], True, True, [PR.ht[(h // 2, 0)], Pm.r], [bD.r])
                for h in range(4):
                    mm(bE.t[:, h * 128:(h + 1) * 128], Pm.t[:, h, :], PR.t[:, h, 1, :], True, True, [PR.ht[(h // 2, 1)], Pm.r], [bE.r])
                act(Pm.t, h4(bD.t[:]), AF.Copy, [bD.r], [Pm.r])
                tt(PR.t[:, :, 1, :], h4(bE.t[:]), PR.t[:, :, 1, :], ALU.add, [bE.r, PR.ht[(0, 1)], PR.ht[(1, 1)]], [PR.ht[(0, 1)], PR.ht[(1, 1)]])
            elif lvl == 5:
                bE = nb()
                for h in range(4):
                    mm(bE.t[:, h * 128:(h + 1) * 128], Pm.t[:, h, :], PR.t[:, h, 1, :], True, True, [PR.ht[(h // 2, 1)], Pm.r], [bE.r])
                tt(TTb.t[:], h4(bE.t[:]), PR.t[:, :, 1, :], ALU.add, [bE.r, PR.r, PR.ht[(0, 1)], PR.ht[(1, 1)]], [TTb.r])
            else:
                Kbg, nWt = KbgS[ci], nWtS[ci]
                dbg("TTb", TTb.t[:], TTb.r, ci == 0)
                bI = nb()
                for h in range(4):
                    mm(bI.t[:, h * 128:(h + 1) * 128], Kbg.t[:, h, :], TTb.t[:, h, :], True, True, [Kbg.r, TTb.r], [bI.r])
                act(nWt.t[:], h4(bI.t[:]), AF.Copy, [bI.r], [nWt.r], scale=-1.0)

        def gen_gdn_state(ci, full, eGl):
            cs = slice(ci * 128, (ci + 1) * 128)
            TTb, Kd, Vb, attnT, Qd, nWt = TTbS[ci], KdS[ci], VbS[ci], attnTS[ci], QdS[ci], nWtS[ci]
            bJ = nb()
            bJ.pinned = True
            bK = nb() if full else None
            if full:
                bK.pinned = True
            for s in range(2):
                rs = slice(s * 64, (s + 1) * 64)
                for h in range(4):
                    o = bJ.t[rs, h * 128:(h + 1) * 128]
                    mm(o, TTb.t[rs, h, rs], Vb.t[rs, h, :], True, False, [TTb.r, Vb.r], [bJ.r])
                    mm(o, nWt.t[:, h, rs], Sb.t[:, h, :], False, True, [nWt.r, Sb.r], [bJ.r])
                yield
                act(Vn.t[rs, :, :], h4(bJ.t[rs, :]), AF.Copy, [bJ.r], [Vn.r])
                if s == 1:
                    bJ.pinned = False
                yield
                if full:
                    for h in range(4):
                        o = bK.t[:, h * 128 + s * 64:h * 128 + (s + 1) * 64]
                        mm(o, Sb.t[:, h, :], Qd.t[:, h, rs], True, False, [Sb.r, Qd.r], [bK.r])
                        mm(o, Vn.t[rs, h, :], attnT.t[rs, h, rs], False, True, [Vn.r, attnT.r], [bK.r])
                bL = nb()
                bL.pinned = True
                for h in range(4):
                    mm(bL.t[:, h * 128:(h + 1) * 128], Kd.t[rs, h, :], Vn.t[rs, h, :], True, True, [Kd.r, Vn.r], [bL.r])
                yield
                for h in range(4):
                    col = s * 16 + ci * 4 + h
                    stt(S.t[:, h, :], S.t[:, h, :], eGl.t[:, col:col + 1], bL.t[:, h * 128:(h + 1) * 128], ALU.mult, ALU.add,
                        [S.rh[h], eGl.r, bL.r], [S.rh[h]])
                bL.pinned = False
                yield
                act(Sb.t[:], S.t[:], AF.Copy, [S.r], [Sb.r])
                if s == 0:
                    yield
            if full:
                act(Ot.t[:, :, cs], h4(bK.t[:]), AF.Copy, [bK.r], [Ot.r])
                bK.pinned = False
            yield

        def stage_mid(full, skip_fronts=False):
            if not skip_fronts:
                for ci in range(4):
                    gdn_front(ci, full)
            for lvl in range(7):
                for ci in range(4):
                    gdn_level(ci, lvl)

        def gen_fronts(full):
            for ci in range(4):
                gdn_front(ci, full)
                yield

        fronts_done = set()

        def gen_states(full, eGl):
            for ci in range(4):
                yield from gen_gdn_state(ci, full, eGl)

        def merge(ga, gb, ra, rb):
            da = db = False
            while not (da and db):
                for _ in range(ra):
                    if not da:
                        try:
                            next(ga)
                        except StopIteration:
                            da = True
                for _ in range(rb):
                    if not db:
                        try:
                            next(gb)
                        except StopIteration:
                            db = True

        def gen_empty():
            return
            yield

        def stage_gdn_out():
            bks = []
            for h in range(4):
                bks.append(rms_sumsq([(Ot.t[:, h, :], Ot.r)], 1))
            for h in range(4):
                rms_ln(bks[h], 1.0 / 128.0, RMS_EPS)
            for h in range(4):
                rms_exp(bks[h])
            for h in range(4):
                stt(raw[h].t[:], bks[h].t[:], vecs.t[:, V_GNW:V_GNW + 1], zs.t[:, h, :], ALU.mult, ALU.mult, [bks[h].r, vecs.r, zs.r], [raw[h].r])
                tt(yT.t[:, h, :], Ot.t[:, h, :], raw[h].t[:], ALU.mult, [Ot.r, raw[h].r], [yT.r])

        def stage_out(xb):
            for pc in range(2):
                for cc in range(4):
                    dc = pc * 4 + cc
                    if cc % 2 == 0:
                        slot = piece_rows("wout", pc * 128, 8, 512, (cc // 2) * 256, 256)
                        wv = slot.wv
                    bk = proj_chunk(wv, cc % 2, slot, yT, 8)
                    if cc % 2 == 1:
                        slot.pinned = False
                    stt(xb.t[:, dc, :], bk.t[:], gate1[:, dc:dc + 1], xb.t[:, dc, :], ALU.mult, ALU.add, [bk.r, mod.r, xb.r], [xb.r])

        def gen_ffn(xb):
            for i in range(6):
                w = 512 if i < 5 else 256
                for hf_ in range(w // 256):
                    sg_ = piece_rows("wg", i * 128, 8, w, hf_ * 256, 256)
                    su_ = piece_rows("wu", i * 128, 8, w, hf_ * 256, 256)
                    for c2 in range(2):
                        fc = i * 4 + hf_ * 2 + c2
                        bg = proj_chunk(sg_.wv, c2, sg_, h2T, 8)
                        bu = proj_chunk(su_.wv, c2, su_, h2T, 8)
                        if c2 == 1:
                            sg_.pinned = False
                            su_.pinned = False
                        s_ = rot(sgb, "sg")
                        act(s_.t[:], bg.t[:], AF.Silu, [bg.r], [s_.r])
                        tt(actv[fc].t, bu.t[:], s_.t[:], ALU.mult, [bu.r, s_.r], [actv[fc].r])
                        yield
            for dc in range(8):
                slA = piece(D["wd"][dc * 128:(dc + 1) * 128, 0:11 * 128].rearrange("p (j n) -> p j n", j=11), 11, 128)
                slB = piece(D["wd"][dc * 128:(dc + 1) * 128, 11 * 128:22 * 128].rearrange("p (j n) -> p j n", j=11), 11, 128)
                bk = nb()
                for fc in range(22):
                    sl = slA if fc < 11 else slB
                    mm(bk.t[:], sl.wv[:, fc % 11, :], actv[fc].t, fc == 0, fc == 21, [sl.r, actv[fc].r], [bk.r])
                slA.pinned = False
                slB.pinned = False
                stt(xb.t[:, dc, :], bk.t[:], gate2[:, dc:dc + 1], xb.t[:, dc, :], ALU.mult, ALU.add, [bk.r, mod.r, xb.r], [xb.r])
                yield

        def stage_final(xb, ti):
            rb = rms_rstd([(xb.t[:, j, :], xb.r) for j in range(8)], 8, 1.0 / 1024.0, RMS_EPS)
            for j in range(8):
                stt(xb.t[:, j, :], xb.t[:, j, :], vecs.t[:, V_NF + j:V_NF + j + 1], rb.t[:], ALU.mult, ALU.mult,
                    [xb.r, vecs.r, rb.r], [xb.r])
            dstv = outd.rearrange("(j p) t -> p j t", p=128)[:, :, ti * TT:(ti + 1) * TT]
            fw.dma(SP, out_ds, lambda e: e.dma_start(out=dstv, in_=xb.t[:]), [xb.r], [])
            return None

        seq = [("xp", ti, False) for ti in range(NT)] + [("xm", ti, True) for ti in range(NT)]
        if DBG["seq"] is not None:
            seq = DBG["seq"]
        dbg("mod", mod.t[:], mod.r)
        def gen_pre(n):
            src, ti, is_main = seq[n]
            xb = xs[n % 2]
            full = is_main or (ti == NT - 1)
            yield from gen_norm(xb, a1, shift1, hT)
            cur["n"] = n
            dbg("hT", hT.t[:], hT.r)
            yield from gen_proj_qkv(full)
            cur["n"] = n
            dbg("kn", kn.t[:], kn.r); dbg("vT", vT.t[:], vT.r); dbg("qn", qn.t[:], qn.r)
            stage_gates(absb, not is_main, eGlS[n % 2])
            for nm in ["g", "beta", "cP", "skbg", "skd", "cA"]:
                dbg("gt_" + nm, gt[nm].t[:], gt[nm].r)
            dbg("gsb", gsb.t[:], gsb.r)
            yield

        def gen_halo_mask():
            ts(halo.t[:], halo.t[:], vecs.t[:, V_PM:V_PM + 1], ALU.mult, [halo.r, vecs.r], [halo.r])
            yield

        def chain(*gens):
            for g_ in gens:
                yield from g_

        load_x(D[seq[0][0]], seq[0][1], 0)
        for _ in gen_pre(0):
            pass
        for n, (src, ti, is_main) in enumerate(seq):
            xb = xs[n % 2]
            cur["n"] = n
            has_next = n + 1 < len(seq)
            if has_next:
                load_x(D[seq[n + 1][0]], seq[n + 1][1], (n + 1) % 2)
            full = is_main or (ti == NT - 1)
            stage_mid(is_main, skip_fronts=(n in fronts_done))
            nxt = gen_pre(n + 1) if has_next else gen_empty()
            if is_main and has_next and DBG.get("ff", True):
                nxt = chain(nxt, gen_fronts(True))
                fronts_done.add(n + 1)
            if is_main:
                merge(gen_states(True, eGlS[n % 2]), gen_proj_rest(), 5, 2)
                cur["n"] = n
                dbg("zs", zs.t[:], zs.r); dbg("ysc", yT.t[:], yT.r)
                dbg("S", S.t[:], S.r)
                dbg("Ot", Ot.t[:], Ot.r)
                stage_gdn_out()
                dbg("yT", yT.t[:], yT.r)
                stage_out(xb)
                dbg("x1", xb.t[:], xb.r)
                for _ in gen_norm(xb, a2, shift2, h2T):
                    pass
                merge(gen_ffn(xb), nxt, 1, 2)
                cur["n"] = n
                dbg("x2", xb.t[:], xb.r)
                stage_final(xb, ti)
            else:
                if full:
                    others = chain(gen_proj_rest(), gen_halo_mask(), nxt)
                else:
                    others = nxt
                merge(gen_states(False, eGlS[n % 2]), others, 2, 1)
        SP.prog.append(("wait", out_ds.sem, out_ds.count))
        if dbg_ds.count:
            SP.prog.append(("wait", dbg_ds.sem, dbg_ds.count))
        fw.finish()
    return nc


def _piece(W, c0, c1, nk=8):
    w = np.ascontiguousarray(W[:, c0:c1])
    n = c1 - c0
    return np.ascontiguousarray(w.reshape(nk, 128, n).transpose(1, 0, 2)).reshape(128, nk * n)


def _pad(a, n):
    out = np.zeros((a.shape[0], n), np.float32)
    out[:, :a.shape[1]] = a
    return out


def _fm(v):
    return np.ascontiguousarray(np.asarray(v, np.float32).reshape(-1, 128).T)


_CACHE = {}


def kernel(x, c, w_ada, b_ada, norm1_w, w_in, conv_qkv_w, A_log, dt_bias, gdn_norm_w,
           conv_short_w, w_out, norm2_w, w_gate, w_up, w_down, norm_f_w):
    f = lambda a: np.asarray(a, np.float32)
    x, c = f(x), f(c)
    w_ada, b_ada, w_in, w_out = f(w_ada)[0], f(b_ada)[0], f(w_in)[0], f(w_out)[0]
    w_gate, w_up, w_down = f(w_gate)[0], f(w_up)[0], f(w_down)[0]
    idx = np.arange(128)
    same = (idx[:, None] // 64) == (idx[None, :] // 64)
    idf = np.eye(128, dtype=np.float32)
    tri = (same & (idx[:, None] <= idx[None, :])).astype(np.float32)
    blk = same.astype(np.float32)
    indA = np.repeat((idx < 64).astype(np.float32)[:, None], 128, 1)
    indB = np.repeat((idx >= 64).astype(np.float32)[:, None], 128, 1)
    mnegs = np.where(same & (idx[:, None] > idx[None, :]), 0.0, NEG).astype(np.float32)
    mnegi = np.where(same & (idx[None, :] >= idx[:, None]), 0.0, NEG).astype(np.float32)
    consts = np.concatenate([idf, tri, blk, indA, indB, mnegs, mnegi], 1)
    wada = np.concatenate([_piece(w_ada, p * 512, (p + 1) * 512) for p in range(12)], 0)
    cols = [(0, 512), (512, 1024), (1024, 1536), (1536, 2048), (2568, 3080), (3080, 3592), (2056, 2568)]
    win = np.concatenate([_piece(w_in, a, b) for a, b in cols], 0)
    wab = _piece(w_in, 2048, 2056)
    wout = np.concatenate([_piece(w_out, p * 512, (p + 1) * 512) for p in range(2)], 0)
    gcols = [(i * 512, min((i + 1) * 512, 2816)) for i in range(6)]
    wg = np.concatenate([_pad(_piece(w_gate, a, b), 4096) for a, b in gcols], 0)
    wu = np.concatenate([_pad(_piece(w_up, a, b), 4096) for a, b in gcols], 0)
    wd = np.concatenate([_piece(w_down, d * 128, (d + 1) * 128, nk=22) for d in range(8)], 0)
    cw = f(conv_qkv_w)[0]
    cs = f(conv_short_w)[0]
    cwT = np.ascontiguousarray(cw.reshape(4, 12, 128).transpose(2, 1, 0)).reshape(128, 48)
    csT = np.ascontiguousarray(cs.reshape(3, 4, 128).transpose(2, 1, 0)).reshape(128, 12)
    in_maps = []
    for r in range(8):
        b, hf = r // 2, r % 2
        vecs = np.zeros((128, NV), np.float32)
        vecs[:, V_C:V_C + 8] = _fm(c[b])
        vecs[:, V_BADA:V_BADA + 48] = _fm(b_ada)
        vecs[:, V_N1:V_N1 + 8] = _fm(f(norm1_w)[0])
        vecs[:, V_N2:V_N2 + 8] = _fm(f(norm2_w)[0])
        vecs[:, V_NF:V_NF + 8] = _fm(f(norm_f_w))
        vecs[:, V_CW:V_CW + 48] = cwT
        vecs[:, V_CS:V_CS + 12] = csT
        vecs[:, V_ALOG:V_ALOG + 4] = f(A_log)[0][None, :]
        vecs[:, V_DTB:V_DTB + 4] = f(dt_bias)[0][None, :]
        vecs[:, V_GNW] = f(gdn_norm_w)[0]
        vecs[:, V_PM] = float(hf)
        in_maps.append({
            "xm": np.ascontiguousarray(x[b, hf * 4096:(hf + 1) * 4096].T),
            "xp": np.ascontiguousarray(x[b, 0:4096].T),
            "consts": consts, "vecs": vecs, "wada": wada, "win": win, "wab": wab, "wout": wout,
            "wg": wg, "wu": wu, "wd": wd,
        })
    if "nc" not in _CACHE:
        _CACHE["nc"] = build_program()
    if DBG["on"]:
        res = run_bass_kernel_spmd(_CACHE["nc"], in_maps[:DBG.get("ncores", 8)], core_ids=list(range(DBG.get("ncores", 8))))
        DBG["res"] = res.results
        return None
    res = run_bass_kernel_spmd(_CACHE["nc"], in_maps, core_ids=list(range(8)))
    out = np.empty((4, 8192, 1024), np.float32)
    for r in range(8):
        b, hf = r // 2, r % 2
        out[b, hf * 4096:(hf + 1) * 4096] = res.results[r]["out"].T
    return out
```

```python
import numpy as np
from contextlib import ExitStack
import concourse.bass as bass
import concourse.mybir as mybir
from concourse.bass_utils import run_bass_kernel_spmd

F32 = mybir.dt.float32
BF16 = mybir.dt.bfloat16
AF = mybir.ActivationFunctionType
ALU = mybir.AluOpType

TT = 512
NT = 8
NEG = -30000.0
RMS_EPS = 1e-6
L2_EPS = 1e-6

V_C, V_BADA, V_N1, V_N2, V_NF, V_CW, V_CS, V_ALOG, V_DTB, V_GNW, V_PM = 0, 8, 56, 64, 72, 80, 128, 140, 144, 148, 149
NV = 150


class Tk:
    __slots__ = ("sem", "val", "key")

    def __init__(self, sem, val, key):
        self.sem, self.val, self.key = sem, val, key


class Res:
    __slots__ = ("w", "r")

    def __init__(self):
        self.w = None
        self.r = {}


class Eng:
    def __init__(self, name, sem):
        self.name, self.sem = name, sem
        self.count = 0
        self.waited = {}
        self.prog = []


class DSem:
    def __init__(self, sem, key):
        self.sem, self.key, self.count = sem, key, 0


class FW:
    def __init__(self, nc, stack):
        self.nc, self.stack = nc, stack
        mk = lambda n: Eng(n, stack.enter_context(nc.semaphore("s_" + n)))
        self.PE, self.ACT, self.DVE, self.POOL, self.SP = mk("pe"), mk("act"), mk("dve"), mk("pool"), mk("sp")
        self.nds = 0

    def dsem(self):
        self.nds += 1
        return DSem(self.stack.enter_context(self.nc.semaphore("s_d%d" % self.nds)), "d%d" % self.nds)

    def sb(self, name, shape, dt):
        return self.stack.enter_context(self.nc.sbuf_tensor(name, list(shape), dt))

    def ps(self, name, shape, dt):
        return self.stack.enter_context(self.nc.psum_tensor(name, list(shape), dt))

    def _wait(self, E, t):
        if t.key == E.name and E.name == "pe":
            return
        if E.waited.get(t.key, 0) >= t.val:
            return
        E.prog.append(("wait", t.sem, t.val))
        E.waited[t.key] = t.val

    @staticmethod
    def _flat(lst):
        out = []
        for b in lst:
            if isinstance(b, (list, tuple)):
                out.extend(FW._flat(b))
            elif b is not None:
                out.append(b)
        return out

    def _deps(self, E, reads, writes):
        reads, writes = self._flat(reads), self._flat(writes)
        for b in reads:
            if b.w is not None:
                self._wait(E, b.w)
        for b in writes:
            if b.w is not None and b.w.key != E.name:
                self._wait(E, b.w)
            for t in b.r.values():
                if t.key != E.name:
                    self._wait(E, t)

    def _mark(self, tk, reads, writes):
        reads, writes = self._flat(reads), self._flat(writes)
        for b in reads:
            o = b.r.get(tk.key)
            if o is None or o.val < tk.val:
                b.r[tk.key] = tk
        for b in writes:
            b.w = tk
            b.r = {}

    def op(self, E, fn, reads=(), writes=()):
        self._deps(E, reads, writes)
        E.count += 1
        tk = Tk(E.sem, E.count, E.name)
        E.prog.append(("op", fn))
        self._mark(tk, reads, writes)
        return tk

    def dma(self, E, ds, fn, reads=(), writes=()):
        self._deps(E, reads, writes)
        ds.count += 16
        tk = Tk(ds.sem, ds.count, ds.key)
        E.prog.append(("dma", fn, ds.sem))
        self._mark(tk, reads, writes)
        return tk

    def finish(self):
        engs = [(self.PE, "tensor"), (self.ACT, "scalar"), (self.DVE, "vector"), (self.POOL, "gpsimd"), (self.SP, "sync")]
        with self.nc.Block() as block:
            for E, attr in engs:
                def body(eng, E=E):
                    for it in E.prog:
                        if it[0] == "wait":
                            eng.wait_ge(it[1], it[2])
                        elif it[0] == "op":
                            it[1](eng).then_inc(E.sem, 1)
                        else:
                            it[1](eng).then_inc(it[2], 16)
                getattr(block, attr)(body)


class B:
    def __init__(self, t, nres=1):
        self.t = t
        if nres == 1:
            self.r = Res()
        else:
            self.rh = [Res() for _ in range(nres)]
            self.r = tuple(self.rh)

    def __getitem__(self, k):
        return self.t[k]


DBG = {"on": False, "seq": None, "names": []}


def build_program():
    nc = bass.Bass("TRN2", target_bir_lowering=False)
    D = {}
    DBG["names"] = []

    def din(name, shape):
        D[name] = nc.dram_tensor(name, list(shape), F32, kind="ExternalInput").ap()

    din("xm", [1024, 4096]); din("xp", [1024, 4096])
    din("consts", [128, 7 * 128]); din("vecs", [128, NV])
    din("wada", [12 * 128, 4096]); din("win", [7 * 128, 4096]); din("wab", [128, 64])
    din("wout", [2 * 128, 4096]); din("wg", [6 * 128, 4096]); din("wu", [6 * 128, 4096]); din("wd", [8 * 128, 2816])
    outd = nc.dram_tensor("out", [1024, 4096], F32, kind="ExternalOutput").ap()

    st = ExitStack()
    with st:
        fw = FW(nc, st)
        PE, ACT, DVE, POOL, SP = fw.PE, fw.ACT, fw.DVE, fw.POOL, fw.SP

        def sb(name, shape, dt=F32, nres=1):
            return B(fw.sb("sb_" + name, shape, dt), nres)

        def mm(out, lhsT, rhs, start, stop, rd, wr):
            fw.op(PE, lambda e: e.matmul(out, lhsT=lhsT, rhs=rhs, start=start, stop=stop), rd, wr)

        def tr(out, in_, ident, rd, wr):
            fw.op(PE, lambda e: e.transpose(out, in_, ident), rd, wr)

        def act(out, in_, func, rd, wr, bias=None, scale=None):
            kw = {}
            if bias is not None:
                kw["bias"] = bias
            if scale is not None:
                kw["scale"] = scale
            fw.op(ACT, lambda e: e.activation(out=out, in_=in_, func=func, **kw), rd, wr)

        def tt(out, in0, in1, op, rd, wr, E=None):
            fw.op(E or DVE, lambda e: e.tensor_tensor(out=out, in0=in0, in1=in1, op=op), rd, wr)

        def ts(out, in0, s1, op0, rd, wr, s2=None, op1=None, E=None):
            if op1 is None:
                fw.op(E or DVE, lambda e: e.tensor_scalar(out=out, in0=in0, scalar1=s1, scalar2=None, op0=op0), rd, wr)
            else:
                fw.op(E or DVE, lambda e: e.tensor_scalar(out=out, in0=in0, scalar1=s1, scalar2=s2, op0=op0, op1=op1), rd, wr)

        def stt(out, in0, scalar, in1, op0, op1, rd, wr):
            fw.op(DVE, lambda e: e.scalar_tensor_tensor(out=out, in0=in0, scalar=scalar, in1=in1, op0=op0, op1=op1), rd, wr)

        def cp(out, in_, rd, wr, E=None):
            fw.op(E or DVE, lambda e: e.tensor_copy(out=out, in_=in_), rd, wr)

        def recip(out, in_, rd, wr):
            fw.op(DVE, lambda e: e.reciprocal(out=out, in_=in_), rd, wr)

        def h4(ap):
            return ap.rearrange("p (h c) -> p h c", h=4)

        consts = sb("consts", [128, 7 * 128])
        idf = consts.t[:, 0:128]; tri = consts.t[:, 128:256]; blk = consts.t[:, 256:384]
        indA = consts.t[:, 384:512]; indB = consts.t[:, 512:640]; mnegs = consts.t[:, 640:768]; mnegi = consts.t[:, 768:896]
        vecs = sb("vecs", [128, NV])
        idb = sb("idb", [128, 128], BF16)
        ones = sb("ones", [128, 128], BF16)
        mod = sb("mod", [128, 64])
        cact = sb("cact", [128, 8], BF16)
        ctmp = sb("ctmp", [128, 16])
        negA = sb("negA", [128, 4])
        wab = sb("wab", [128, 64], BF16)
        xs = [sb("xs%d" % i, [128, 8, TT]) for i in range(2)]
        xds = [fw.dsem() for _ in range(2)]
        hT = sb("hT", [128, 8, TT], BF16)
        h2T = sb("h2T", [128, 8, TT], BF16)
        yT = sb("yT", [128, 8, TT], BF16)
        sqb = [sb("sqb%d" % i, [128, TT], BF16) for i in range(2)]
        rstd = sb("rstd", [128, TT]); lnv = rstd
        tmp = [sb("tmp%d" % i, [128, TT]) for i in range(1)]
        NSLOT = 6
        wr_ = [sb("wr%d" % i, [128, 2048], BF16) for i in range(NSLOT)]
        for w_ in wr_:
            w_.pinned = False
        wds = [fw.dsem() for _ in range(NSLOT)]
        qn = sb("qn", [128, 4, TT], BF16); kn = sb("kn", [128, 4, TT], BF16); vT = sb("vT", [128, 4, TT], BF16)
        zs = sb("zs", [128, 4, TT], BF16)
        raw = [sb("raw%d" % i, [128, TT]) for i in range(4)]
        pin = [sb("pin%d" % i, [128, TT + 3]) for i in range(1)]
        acc = [sb("acc%d" % i, [128, TT]) for i in range(2)]
        halo = sb("halo", [128, 16, 3])
        sccs = sb("sccs", [128, 4, TT], BF16); cvo = sb("cvo", [128, 4, TT], BF16)
        sgb = [sb("sgb%d" % i, [128, TT]) for i in range(2)]
        arena = fw.sb("sb_arena", [128, 6144], F32)
        pages = [Res() for _ in range(24)]

        class V:
            def __init__(self, t, r):
                self.t, self.r = t, r
        arena_bf = arena[:].bitcast(BF16)
        actv = [V(arena_bf[:, fc * 512:(fc + 1) * 512], pages[fc]) for fc in range(22)]
        Ot = sb("Ot", [128, 4, TT], BF16)
        S = sb("S", [128, 4, 128], nres=4); Sb = sb("Sb", [128, 4, 128], BF16)
        gt = {n: sb("g_" + n, [128, 16]) for n in ["xa", "ax", "e1", "l1", "g", "ng", "eb", "opb", "beta", "lnopb", "cP", "cA", "skbg", "skd", "t1"]}
        gsb = sb("gsb", [128, 64]); eGlS = [sb("eGl%d" % i, [128, 32]) for i in range(2)]; absb = sb("absb", [128, 32])
        nrepS = [sb("nrep%d" % i, [128, 4, 128], nres=4) for i in range(2)]
        EsS = [sb("Es%d" % i, [128, 4, 128], nres=4) for i in range(2)]
        PmS = [V(arena[:, c * 1536:c * 1536 + 512].rearrange("p (h c) -> p h c", h=4), tuple(pages[6 * c:6 * c + 2])) for c in range(4)]
        PRS = [V(arena[:, c * 1536 + 512:(c + 1) * 1536].rearrange("p (h t c) -> p h t c", h=4, t=2), tuple(pages[6 * c + 2:6 * c + 6])) for c in range(4)]
        for pr_ in PRS:
            pr_.ht = {(hp, t_): Res() for hp in range(2) for t_ in range(2)}
        TTbS = [sb("TTb%d" % i, [128, 4, 128], BF16) for i in range(4)]
        KbgS = [sb("Kbg%d" % i, [128, 4, 128], BF16) for i in range(4)]
        KdS = [sb("Kd%d" % i, [128, 4, 128], BF16) for i in range(4)]
        VbS = [sb("Vb%d" % i, [128, 4, 128], BF16) for i in range(4)]
        attnTS = [sb("attnT%d" % i, [128, 4, 128], BF16) for i in range(4)]
        QdS = [sb("Qd%d" % i, [128, 4, 128], BF16) for i in range(4)]
        nWtS = [sb("nWt%d" % i, [128, 4, 128], BF16) for i in range(4)]
        Vn = sb("Vn", [128, 4, 128], BF16)
        banks = [B(fw.ps("pb%d" % i, [128, 512], F32)) for i in range(8)]
        bstate = {"i": 0, "w": 0, "x": 0, "s": 0, "t": 0, "pin": 0, "acc": 0, "sg": 0}

        def nb():
            b = banks[bstate["i"] % 8]
            bstate["i"] += 1
            return b

        def rot(lst, key):
            b = lst[bstate[key] % len(lst)]
            bstate[key] += 1
            return b

        misc_ds = fw.dsem()
        out_ds = fw.dsem()
        dbg_ds = fw.dsem()
        cur = {"n": -1}

        def dbg(tag, ap, res, cond=True):
            if not (DBG["on"] and cond):
                return
            name = "dbg_%s_%d" % (tag, cur["n"])
            if name in DBG["names"]:
                return
            DBG["names"].append(name)
            shp = list(ap.shape)
            dt_ = ap.dtype
            dr = nc.dram_tensor(name, shp, dt_, kind="ExternalOutput").ap()
            fw.dma(SP, dbg_ds, lambda e: e.dma_start(out=dr, in_=ap), [res], [])

        def piece(src_ap, nj, n):
            for k in range(NSLOT):
                i = (bstate["w"] + k) % NSLOT
                if not wr_[i].pinned:
                    break
            else:
                raise RuntimeError("all weight slots pinned")
            bstate["w"] = i + 1
            slot = wr_[i]
            slot.pinned = True
            dst = slot.t[:, 0:nj * n].rearrange("p (j n) -> p j n", j=nj)
            fw.dma(POOL, wds[i], lambda e: e.dma_start(out=dst, in_=src_ap, max_dma_last_dim=4096), [], [slot.r])
            slot.wv = dst
            return slot

        def piece_rows(dname, row0, nj, ncols, c0, n):
            src = D[dname][row0:row0 + 128, 0:nj * ncols].rearrange("p (j n) -> p j n", j=nj)[:, :, c0:c0 + n]
            return piece(src, nj, n)

        fw.dma(SP, misc_ds, lambda e: e.dma_start(out=consts.t[:], in_=D["consts"]), [], [consts.r])
        misc_ds2 = fw.dsem()
        fw.dma(SP, misc_ds2, lambda e: e.dma_start(out=vecs.t[:], in_=D["vecs"]), [], [vecs.r])
        wabds = fw.dsem()
        fw.dma(POOL, wabds, lambda e: e.dma_start(out=wab.t[:], in_=D["wab"]), [], [wab.r])
        cp(idb.t[:], idf, [consts.r], [idb.r])
        fw.op(DVE, lambda e: e.memset(ones.t[:], 1.0), [], [ones.r])
        fw.op(DVE, lambda e: e.memset(halo.t[:], 0.0), [], [halo.r])
        fw.op(DVE, lambda e: e.memset(S.t[:], 0.0), [], [S.r])
        fw.op(DVE, lambda e: e.memset(Sb.t[:], 0.0), [], [Sb.r])
        fw.op(DVE, lambda e: e.memset(Vn.t[:], 0.0), [], [Vn.r])
        act(ctmp.t[:, 0:8], vecs.t[:, V_C:V_C + 8], AF.Exp, [vecs.r], [ctmp.r], scale=-1.0)
        ts(ctmp.t[:, 0:8], ctmp.t[:, 0:8], 1.0, ALU.add, [ctmp.r], [ctmp.r])
        recip(ctmp.t[:, 0:8], ctmp.t[:, 0:8], [ctmp.r], [ctmp.r])
        tt(cact.t[:], ctmp.t[:, 0:8], vecs.t[:, V_C:V_C + 8], ALU.mult, [ctmp.r, vecs.r], [cact.r])
        act(negA.t[:], vecs.t[:, V_ALOG:V_ALOG + 4], AF.Exp, [vecs.r], [negA.r])
        ts(negA.t[:], negA.t[:], -1.0, ALU.mult, [negA.r], [negA.r])
        mb = nb()
        for pc in range(12):
            for hf_ in range(2):
                slot = piece_rows("wada", pc * 128, 8, 512, hf_ * 256, 256)
                wv = slot.wv
                for o2 in range(2):
                    col = pc * 4 + hf_ * 2 + o2
                    for j in range(8):
                        mm(mb.t[:, col:col + 1], wv[:, j, o2 * 128:(o2 + 1) * 128], cact.t[:, j:j + 1], j == 0, j == 7,
                           [slot.r, cact.r], [mb.r])
                slot.pinned = False
        tt(mod.t[:, 0:48], mb.t[:, 0:48], vecs.t[:, V_BADA:V_BADA + 48], ALU.add, [mb.r, vecs.r], [mod.r])
        dbg("vecs", vecs.t[:], vecs.r)
        stt(mod.t[:, 48:56], mod.t[:, 8:16], 1.0, vecs.t[:, V_N1:V_N1 + 8], ALU.add, ALU.mult, [mod.r, vecs.r], [mod.r])
        stt(mod.t[:, 56:64], mod.t[:, 32:40], 1.0, vecs.t[:, V_N2:V_N2 + 8], ALU.add, ALU.mult, [mod.r, vecs.r], [mod.r])
        shift1, gate1, shift2, gate2 = mod.t[:, 0:8], mod.t[:, 16:24], mod.t[:, 24:32], mod.t[:, 40:48]
        a1, a2 = mod.t[:, 48:56], mod.t[:, 56:64]

        def load_x(src, ti, slot_i):
            xb = xs[slot_i]
            srcv = src.rearrange("(j p) t -> p j t", p=128)[:, :, ti * TT:(ti + 1) * TT]
            fw.dma(SP, xds[slot_i], lambda e: e.dma_start(out=xb.t[:], in_=srcv), [], [xb.r])

        def rms_rstd(xb, nj, width_scale, eps, extra_bias=None):
            bk = nb()
            for j, (ap, r) in enumerate(xb):
                sq = rot(sqb, "s")
                act(sq.t[:], ap, AF.Square, [r], [sq.r])
                mm(bk.t[:], ones.t[:], sq.t[:], j == 0, j == nj - 1, [ones.r, sq.r], [bk.r])
            act(lnv.t[:], bk.t[:], AF.Ln, [bk.r], [lnv.r], bias=eps, scale=width_scale)
            if extra_bias is None:
                act(rstd.t[:], lnv.t[:], AF.Exp, [lnv.r], [rstd.r], scale=-0.5)
            else:
                act(rstd.t[:], lnv.t[:], AF.Exp, [lnv.r], [rstd.r], scale=-0.5, bias=extra_bias)

        def gen_norm(xb, a_, sh_, dst):
            rms_rstd([(xb.t[:, j, :], xb.r) for j in range(8)], 8, 1.0 / 1024.0, RMS_EPS)
            yield
            for j in range(8):
                t_ = rot(tmp, "t")
                stt(t_.t[:], xb.t[:, j, :], a_[:, j:j + 1], rstd.t[:], ALU.mult, ALU.mult, [xb.r, mod.r, rstd.r], [t_.r])
                act(dst.t[:, j, :], t_.t[:], AF.Identity, [t_.r, mod.r], [dst.r], bias=sh_[:, j:j + 1])
                yield

        def proj_chunk(wv, cc, slot, rhs_b, nk):
            bk = nb()
            for j in range(nk):
                mm(bk.t[:], wv[:, j, cc * 128:(cc + 1) * 128], rhs_b.t[:, j, :], j == 0, j == nk - 1, [slot.r, rhs_b.r], [bk.r])
            return bk

        def conv_chunk(src_ap, src_r, ch, ntap, wcol0, out_ap, out_r, from_psum_copy=True):
            p_ = rot(pin, "pin")
            hl = ntap - 1
            if from_psum_copy:
                act(p_.t[:, hl:hl + TT], src_ap, AF.Copy, src_r, [p_.r])
            else:
                src_ap(p_.t[:, hl:hl + TT], p_.r)
            cp(p_.t[:, 0:hl], halo.t[:, ch, 0:hl], [halo.r], [p_.r])
            cp(halo.t[:, ch, 0:hl], p_.t[:, TT:TT + hl], [p_.r], [halo.r])
            ts(out_ap, p_.t[:, 0:TT], vecs.t[:, wcol0:wcol0 + 1], ALU.mult, [p_.r, vecs.r], [out_r])
            for k in range(1, ntap):
                stt(out_ap, p_.t[:, k:k + TT], vecs.t[:, wcol0 + k:wcol0 + k + 1], out_ap, ALU.mult, ALU.add,
                    [p_.r, vecs.r, out_r], [out_r])

        def wpiece(i, hf_):
            slot = piece_rows("win", i * 128, 8, 512, hf_ * 256, 256)
            return slot, slot.wv

        def gen_proj_qkv(full):
            for pi in ([0, 1, 2] if full else [1, 2]):
                for cc in range(4):
                    if cc % 2 == 0:
                        slot, wv = wpiece(pi, cc // 2)
                    bk = proj_chunk(wv, cc % 2, slot, hT, 8)
                    if cc % 2 == 1:
                        slot.pinned = False
                    ch = pi * 4 + cc
                    a_ = rot(acc, "acc")
                    conv_chunk(bk.t[:], [bk.r], ch, 4, V_CW + ch * 4, a_.t[:], a_.r)
                    if pi == 2:
                        act(vT.t[:, cc, :], a_.t[:], AF.Silu, [a_.r], [vT.r])
                    else:
                        act(raw[cc].t[:], a_.t[:], AF.Silu, [a_.r], [raw[cc].r])
                    yield
                if pi != 2:
                    dst = qn if pi == 0 else kn
                    for cc in range(4):
                        rms_rstd([(raw[cc].t[:], raw[cc].r)], 1, 1.0, L2_EPS, extra_bias=(-0.5 * np.log(128.0) if pi == 0 else None))
                        tt(dst.t[:, cc, :], raw[cc].t[:], rstd.t[:], ALU.mult, [raw[cc].r, rstd.r], [dst.r])
                        yield
            abk = nb()
            wabv = wab.t[:].rearrange("p (j n) -> p j n", j=8)
            for ci in range(4):
                for j in range(8):
                    mm(abk.t[:, ci * 8:(ci + 1) * 8], hT.t[:, j, ci * 128:(ci + 1) * 128], wabv[:, j, :], j == 0, j == 7,
                       [hT.r, wab.r], [abk.r])
            cp(absb.t[:], abk.t[:, 0:32], [abk.r], [absb.r])
            yield

        def gen_proj_rest():
            for cc in range(4):
                if cc % 2 == 0:
                    slot, wv = wpiece(3, cc // 2)
                bk = proj_chunk(wv, cc % 2, slot, hT, 8)
                if cc % 2 == 1:
                    slot.pinned = False
                act(zs.t[:, cc, :], bk.t[:], AF.Silu, [bk.r], [zs.r])
                yield
            for cc in range(4):
                if cc % 2 == 0:
                    slot, wv = wpiece(4, cc // 2)
                bk = proj_chunk(wv, cc % 2, slot, hT, 8)
                if cc % 2 == 1:
                    slot.pinned = False
                act(sccs.t[:, cc, :], bk.t[:], AF.Copy, [bk.r], [sccs.r])
                yield
            for cc in range(4):
                if cc % 2 == 0:
                    slot, wv = wpiece(5, cc // 2)
                bk = proj_chunk(wv, cc % 2, slot, hT, 8)
                if cc % 2 == 1:
                    slot.pinned = False

                def prod(dst_ap, dst_r, bk=bk, cc=cc):
                    tt(dst_ap, bk.t[:], sccs.t[:, cc, :], ALU.mult, [bk.r, sccs.r], [dst_r])
                conv_chunk(prod, None, 12 + cc, 3, V_CS + cc * 3, cvo.t[:, cc, :], cvo.r, from_psum_copy=False)
                yield
            for cc in range(4):
                if cc % 2 == 0:
                    slot, wv = wpiece(6, cc // 2)
                bk = proj_chunk(wv, cc % 2, slot, hT, 8)
                if cc % 2 == 1:
                    slot.pinned = False
                tt(yT.t[:, 4 + cc, :], bk.t[:], cvo.t[:, cc, :], ALU.mult, [bk.r, cvo.r], [yT.r])
                yield

        def g16(n):
            return gt[n].t[:].rearrange("p (c h) -> p c h", c=4)

        def stage_gates(abk, masked, eGl):
            abv = abk.t[:].rearrange("p (c t h) -> p c t h", c=4, t=2)
            a_ap, b_ap = abv[:, :, 0, :], abv[:, :, 1, :]
            G = lambda n: gt[n]
            dtb = vecs.t[:, V_DTB:V_DTB + 4].unsqueeze(1).to_broadcast([128, 4, 4])
            nAb = negA.t[:].unsqueeze(1).to_broadcast([128, 4, 4])
            tt(g16("xa"), a_ap, dtb, ALU.add, [abk.r, vecs.r], [G("xa").r])
            act(G("ax").t[:], G("xa").t[:], AF.Abs, [G("xa").r], [G("ax").r])
            act(G("e1").t[:], G("ax").t[:], AF.Exp, [G("ax").r], [G("e1").r], scale=-1.0)
            act(G("l1").t[:], G("e1").t[:], AF.Ln, [G("e1").r], [G("l1").r], bias=1.0)
            stt(G("t1").t[:], G("xa").t[:], 0.0, G("l1").t[:], ALU.max, ALU.add, [G("xa").r, G("l1").r], [G("t1").r])
            tt(g16("g"), g16("t1"), nAb, ALU.mult, [G("t1").r, negA.r], [G("g").r])
            ts(G("ng").t[:], G("g").t[:], -1.0, ALU.mult, [G("g").r], [G("ng").r])
            act(g16("eb"), b_ap, AF.Exp, [abk.r], [G("eb").r], scale=-1.0)
            ts(G("opb").t[:], G("eb").t[:], 1.0, ALU.add, [G("eb").r], [G("opb").r])
            recip(G("beta").t[:], G("opb").t[:], [G("opb").r], [G("beta").r])
            act(G("lnopb").t[:], G("opb").t[:], AF.Ln, [G("opb").r], [G("lnopb").r])
            gb = nb()
            gsrc = G("g")
            mm(gb.t[:, 0:16], tri, gsrc.t[:], True, True, [consts.r, gsrc.r], [gb.r])
            mm(gb.t[:, 16:32], blk, gsrc.t[:], True, True, [consts.r, gsrc.r], [gb.r])
            mm(gb.t[:, 32:48], indA, gsrc.t[:], True, True, [consts.r, gsrc.r], [gb.r])
            mm(gb.t[:, 48:64], indB, gsrc.t[:], True, True, [consts.r, gsrc.r], [gb.r])
            cp(gsb.t[:], gb.t[:, 0:64], [gb.r], [gsb.r])
            Gc, Glo = gsb.t[:, 0:16], gsb.t[:, 16:32]
            tt(G("cP").t[:], Gc, G("lnopb").t[:], ALU.subtract, [gsb.r, G("lnopb").r], [G("cP").r])
            act(G("skbg").t[:], G("cP").t[:], AF.Exp, [G("cP").r], [G("skbg").r])
            ts(G("cA").t[:], Gc, -1.0, ALU.mult, [gsb.r], [G("cA").r])
            tt(G("skd").t[:], Glo, Gc, ALU.subtract, [gsb.r], [G("skd").r])
            act(G("skd").t[:], G("skd").t[:], AF.Exp, [G("skd").r], [G("skd").r])
            act(eGl.t[:], gsb.t[:, 32:64], AF.Exp, [gsb.r], [eGl.r])
            if masked:
                pm = vecs.t[:, V_PM:V_PM + 1]
                ts(G("skbg").t[:], G("skbg").t[:], pm, ALU.mult, [G("skbg").r, vecs.r], [G("skbg").r])
                ts(G("beta").t[:], G("beta").t[:], pm, ALU.mult, [G("beta").r, vecs.r], [G("beta").r])

        def bc4(ap_ci):
            return ap_ci.unsqueeze(2).to_broadcast([128, 4, 128])

        def bcm(mask_ap):
            return mask_ap.unsqueeze(1).to_broadcast([128, 4, 128])

        def gdn_front(ci, full):
            cs = slice(ci * 128, (ci + 1) * 128)
            G = lambda n: gt[n]
            nrep = nrepS[ci % 2]; Es = EsS[ci % 2]; Ea = nrep; eGb = Es
            Pm, PR = PmS[ci], PRS[ci]
            Kbg, Kd, Vb, attnT, Qd = KbgS[ci], KdS[ci], VbS[ci], attnTS[ci], QdS[ci]
            cp(nrep.t[:], bc4(g16("ng")[:, ci, :]), [G("ng").r], [nrep.r])
            bA = nb()
            for h in range(4):
                mm(bA.t[:, h * 128:(h + 1) * 128], nrep.t[:, h, :], tri, True, True, [nrep.rh[h], consts.r], [bA.r])
            bB = nb()
            for h in range(4):
                mm(bB.t[:, h * 128:(h + 1) * 128], kn.t[:, h, cs], kn.t[:, h, cs], True, True, [kn.r], [bB.r])
            tt(Es.t[:], h4(bA.t[:]), bcm(mnegs), ALU.add, [bA.r, consts.r], [Es.r])
            for h in range(4):
                act(Es.t[:, h, :], Es.t[:, h, :], AF.Exp, [Es.rh[h], G("cP").r], [Es.rh[h]], bias=g16("cP")[:, ci, h:h + 1])
            stt(Pm.t, h4(bB.t[:]), -1.0, Es.t[:], ALU.mult, ALU.mult, [bB.r, Es.r], [Pm.r])
            dbg("P0", Pm.t, Pm.r, ci == 0)
            bG = nb()
            bGb = bG.t[:].bitcast(BF16)
            for h in range(4):
                tr(bGb[:, h * 128:(h + 1) * 128], kn.t[:, h, cs], idb.t[:], [kn.r, idb.r], [bG.r])
            for h in range(4):
                tr(bGb[:, 512 + h * 128:512 + (h + 1) * 128], vT.t[:, h, cs], idb.t[:], [vT.r, idb.r], [bG.r])
            ktok = bGb[:, 0:512].rearrange("p (h c) -> p h c", h=4)
            vtok = bGb[:, 512:1024].rearrange("p (h c) -> p h c", h=4)
            tt(Kbg.t[:], ktok, bc4(g16("skbg")[:, ci, :]), ALU.mult, [bG.r, G("skbg").r], [Kbg.r])
            tt(Kd.t[:], ktok, bc4(g16("skd")[:, ci, :]), ALU.mult, [bG.r, G("skd").r], [Kd.r])
            tt(Vb.t[:], vtok, bc4(g16("beta")[:, ci, :]), ALU.mult, [bG.r, G("beta").r], [Vb.r])
            if full:
                bH = nb()
                for h in range(4):
                    mm(bH.t[:, h * 128:(h + 1) * 128], kn.t[:, h, cs], qn.t[:, h, cs], True, True, [kn.r, qn.r], [bH.r])
                tt(Ea.t[:], h4(bA.t[:]), bcm(mnegi), ALU.subtract, [bA.r, consts.r], [Ea.r])
                for h in range(4):
                    act(Ea.t[:, h, :], Ea.t[:, h, :], AF.Exp, [Ea.rh[h], G("cA").r], [Ea.rh[h]], bias=g16("cA")[:, ci, h:h + 1], scale=-1.0)
                tt(attnT.t[:], h4(bH.t[:]), Ea.t[:], ALU.mult, [bH.r, Ea.r], [attnT.r])
                act(eGb.t[:], h4(bA.t[:]), AF.Exp, [bA.r], [eGb.r], scale=-1.0)
                tt(Qd.t[:], qn.t[:, :, cs], eGb.t[:], ALU.mult, [qn.r, eGb.r], [Qd.r])
            bC = nb()
            for h in range(4):
                tr(bC.t[:, h * 128:(h + 1) * 128], Pm.t[:, h, :], idf, [Pm.r, consts.r], [bC.r])
            act(PR.t[:, :, 0, :], h4(bC.t[:]), AF.Copy, [bC.r], [PR.r, PR.ht[(0, 0)], PR.ht[(1, 0)]])
            tt(PR.t[:, :, 1, :], h4(bC.t[:]), bcm(idf), ALU.add, [bC.r, consts.r], [PR.r, PR.ht[(0, 1)], PR.ht[(1, 1)]])

        def gdn_level(ci, lvl):
            Pm, PR, TTb = PmS[ci], PRS[ci], TTbS[ci]
            if lvl == 0:
                bD = nb(); bE = nb()
                for h in range(4):
                    mm(bD.t[:, h * 128:(h + 1) * 128], PR.t[:, h, 0, :], Pm.t[:, h, :], True, True, [PR.ht[(h // 2, 0)], Pm.r], [bD.r])
                for h in range(4):
                    mm(bE.t[:, h * 128:(h + 1) * 128], Pm.t[:, h, :], PR.t[:, h, 0, :], True, True, [PR.ht[(h // 2, 0)], Pm.r], [bE.r])
                act(Pm.t, h4(bD.t[:]), AF.Copy, [bD.r], [Pm.r])
                cp(PR.t[:, :, 0, :], h4(bE.t[:]), [bE.r], [PR.ht[(0, 0)], PR.ht[(1, 0)]])
            elif lvl in (1, 2, 3):
                bD = nb(); bE = nb(); bF = nb()
                for h in range(4):
                    mm(bD.t[:, h * 128:(h + 1) * 128], PR.t[:, h, 0, :], Pm.t[:, h, :], True, True, [PR.ht[(h // 2, 0)], Pm.r], [bD.r])
                for h in range(4):
                    bb = bE if h < 2 else bF
                    o = (h % 2) * 256
                    mm(bb.t[:, o:o + 256], Pm.t[:, h, :], PR.t[:, h, :, :].rearrange("p t c -> p (t c)"), True, True,
                       [PR.ht[(h // 2, 0)], PR.ht[(h // 2, 1)], Pm.r], [bb.r])
                act(Pm.t, h4(bD.t[:]), AF.Copy, [bD.r], [Pm.r])
                for hp, (bb, hs) in enumerate(((bE, slice(0, 2)), (bF, slice(2, 4)))):
                    v = bb.t[:].rearrange("p (h t c) -> p h t c", h=2, t=2)
                    tt(PR.t[:, hs, 1, :], v[:, :, 1, :], PR.t[:, hs, 1, :], ALU.add, [bb.r, PR.ht[(hp, 1)]], [PR.ht[(hp, 1)]])
                    act(PR.t[:, hs, 0, :], v[:, :, 0, :], AF.Copy, [bb.r], [PR.ht[(hp, 0)]])
            elif lvl == 4:
                bD = nb(); bE = nb()
                for h in range(4):
                    mm(bD.t[:, h * 128:(h + 1) * 128], PR.t[:, h, 0, :], Pm.t[:, h, :], True, True, [PR.ht[(h // 2, 0)], Pm.r], [bD.r])
                for h in range(4):
                    mm(bE.t[:, h * 128:(h + 1) * 128], Pm.t[:, h, :], PR.t[:, h, 1, :], True, True, [PR.ht[(h // 2, 1)], Pm.r], [bE.r])
                act(Pm.t, h4(bD.t[:]), AF.Copy, [bD.r], [Pm.r])
                tt(PR.t[:, :, 1, :], h4(bE.t[:]), PR.t[:, :, 1, :], ALU.add, [bE.r, PR.ht[(0, 1)], PR.ht[(1, 1)]], [PR.ht[(0, 1)], PR.ht[(1, 1)]])
            elif lvl == 5:
                bE = nb()
                for h in range(4):
                    mm(bE.t[:, h * 128:(h + 1) * 128], Pm.t[:, h, :], PR.t[:, h, 1, :], True, True, [PR.ht[(h // 2, 1)], Pm.r], [bE.r])
                tt(TTb.t[:], h4(bE.t[:]), PR.t[:, :, 1, :], ALU.add, [bE.r, PR.r, PR.ht[(0, 1)], PR.ht[(1, 1)]], [TTb.r])
            else:
                Kbg, nWt = KbgS[ci], nWtS[ci]
                dbg("TTb", TTb.t[:], TTb.r, ci == 0)
                bI = nb()
                for h in range(4):
                    mm(bI.t[:, h * 128:(h + 1) * 128], Kbg.t[:, h, :], TTb.t[:, h, :], True, True, [Kbg.r, TTb.r], [bI.r])
                act(nWt.t[:], h4(bI.t[:]), AF.Copy, [bI.r], [nWt.r], scale=-1.0)

        def gen_gdn_state(ci, full, eGl):
            cs = slice(ci * 128, (ci + 1) * 128)
            TTb, Kd, Vb, attnT, Qd, nWt = TTbS[ci], KdS[ci], VbS[ci], attnTS[ci], QdS[ci], nWtS[ci]
            bJ = nb()
            bK = nb() if full else None
            for s in range(2):
                rs = slice(s * 64, (s + 1) * 64)
                for h in range(4):
                    o = bJ.t[rs, h * 128:(h + 1) * 128]
                    mm(o, TTb.t[:, h, rs], Vb.t[:, h, :], True, False, [TTb.r, Vb.r], [bJ.r])
                    mm(o, nWt.t[:, h, rs], Sb.t[:, h, :], False, True, [nWt.r, Sb.r], [bJ.r])
                act(Vn.t[rs, :, :], h4(bJ.t[rs, :]), AF.Copy, [bJ.r], [Vn.r])
                if full:
                    for h in range(4):
                        o = bK.t[:, h * 128 + s * 64:h * 128 + (s + 1) * 64]
                        mm(o, Sb.t[:, h, :], Qd.t[:, h, rs], True, False, [Sb.r, Qd.r], [bK.r])
                        mm(o, Vn.t[:, h, :], attnT.t[:, h, rs], False, True, [Vn.r, attnT.r], [bK.r])
                bL = nb()
                for h in range(4):
                    mm(bL.t[:, h * 128:(h + 1) * 128], Kd.t[rs, h, :], Vn.t[rs, h, :], True, True, [Kd.r, Vn.r], [bL.r])
                for h in range(4):
                    col = s * 16 + ci * 4 + h
                    stt(S.t[:, h, :], S.t[:, h, :], eGl.t[:, col:col + 1], bL.t[:, h * 128:(h + 1) * 128], ALU.mult, ALU.add,
                        [S.rh[h], eGl.r, bL.r], [S.rh[h]])
                act(Sb.t[:], S.t[:], AF.Copy, [S.r], [Sb.r])
                if s == 0:
                    yield
            if full:
                act(Ot.t[:, :, cs], h4(bK.t[:]), AF.Copy, [bK.r], [Ot.r])
            yield

        def stage_mid(full):
            for ci in range(4):
                gdn_front(ci, full)
            for lvl in range(7):
                for ci in range(4):
                    gdn_level(ci, lvl)

        def gen_states(full, eGl):
            for ci in range(4):
                yield from gen_gdn_state(ci, full, eGl)

        def merge(ga, gb, ra, rb):
            da = db = False
            while not (da and db):
                for _ in range(ra):
                    if not da:
                        try:
                            next(ga)
                        except StopIteration:
                            da = True
                for _ in range(rb):
                    if not db:
                        try:
                            next(gb)
                        except StopIteration:
                            db = True

        def gen_empty():
            return
            yield

        def stage_gdn_out():
            bks = []
            for h in range(4):
                sq = rot(sqb, "s")
                act(sq.t[:], Ot.t[:, h, :], AF.Square, [Ot.r], [sq.r])
                bk = nb()
                mm(bk.t[:], ones.t[:], sq.t[:], True, True, [ones.r, sq.r], [bk.r])
                bks.append(bk)
            for h in range(4):
                act(raw[h].t[:], bks[h].t[:], AF.Ln, [bks[h].r], [raw[h].r], bias=RMS_EPS, scale=1.0 / 128.0)
            for h in range(4):
                act(raw[h].t[:], raw[h].t[:], AF.Exp, [raw[h].r], [raw[h].r], scale=-0.5)
            for h in range(4):
                stt(raw[h].t[:], raw[h].t[:], vecs.t[:, V_GNW:V_GNW + 1], zs.t[:, h, :], ALU.mult, ALU.mult, [raw[h].r, vecs.r, zs.r], [raw[h].r])
                tt(yT.t[:, h, :], Ot.t[:, h, :], raw[h].t[:], ALU.mult, [Ot.r, raw[h].r], [yT.r])

        def stage_out(xb):
            for pc in range(2):
                for cc in range(4):
                    dc = pc * 4 + cc
                    if cc % 2 == 0:
                        slot = piece_rows("wout", pc * 128, 8, 512, (cc // 2) * 256, 256)
                        wv = slot.wv
                    bk = proj_chunk(wv, cc % 2, slot, yT, 8)
                    if cc % 2 == 1:
                        slot.pinned = False
                    stt(xb.t[:, dc, :], bk.t[:], gate1[:, dc:dc + 1], xb.t[:, dc, :], ALU.mult, ALU.add, [bk.r, mod.r, xb.r], [xb.r])

        def gen_ffn(xb):
            for i in range(6):
                w = 512 if i < 5 else 256
                for hf_ in range(w // 256):
                    sg_ = piece_rows("wg", i * 128, 8, w, hf_ * 256, 256)
                    su_ = piece_rows("wu", i * 128, 8, w, hf_ * 256, 256)
                    for c2 in range(2):
                        fc = i * 4 + hf_ * 2 + c2
                        bg = proj_chunk(sg_.wv, c2, sg_, h2T, 8)
                        bu = proj_chunk(su_.wv, c2, su_, h2T, 8)
                        if c2 == 1:
                            sg_.pinned = False
                            su_.pinned = False
                        s_ = rot(sgb, "sg")
                        act(s_.t[:], bg.t[:], AF.Silu, [bg.r], [s_.r])
                        tt(actv[fc].t, bu.t[:], s_.t[:], ALU.mult, [bu.r, s_.r], [actv[fc].r])
                        yield
            for dc in range(8):
                slA = piece(D["wd"][dc * 128:(dc + 1) * 128, 0:11 * 128].rearrange("p (j n) -> p j n", j=11), 11, 128)
                slB = piece(D["wd"][dc * 128:(dc + 1) * 128, 11 * 128:22 * 128].rearrange("p (j n) -> p j n", j=11), 11, 128)
                bk = nb()
                for fc in range(22):
                    sl = slA if fc < 11 else slB
                    mm(bk.t[:], sl.wv[:, fc % 11, :], actv[fc].t, fc == 0, fc == 21, [sl.r, actv[fc].r], [bk.r])
                slA.pinned = False
                slB.pinned = False
                stt(xb.t[:, dc, :], bk.t[:], gate2[:, dc:dc + 1], xb.t[:, dc, :], ALU.mult, ALU.add, [bk.r, mod.r, xb.r], [xb.r])
                yield

        def stage_final(xb, ti):
            rms_rstd([(xb.t[:, j, :], xb.r) for j in range(8)], 8, 1.0 / 1024.0, RMS_EPS)
            for j in range(8):
                stt(xb.t[:, j, :], xb.t[:, j, :], vecs.t[:, V_NF + j:V_NF + j + 1], rstd.t[:], ALU.mult, ALU.mult,
                    [xb.r, vecs.r, rstd.r], [xb.r])
            dstv = outd.rearrange("(j p) t -> p j t", p=128)[:, :, ti * TT:(ti + 1) * TT]
            fw.dma(SP, out_ds, lambda e: e.dma_start(out=dstv, in_=xb.t[:]), [xb.r], [])
            return None

        seq = [("xp", ti, False) for ti in range(NT)] + [("xm", ti, True) for ti in range(NT)]
        if DBG["seq"] is not None:
            seq = DBG["seq"]
        dbg("mod", mod.t[:], mod.r)
        def gen_pre(n):
            src, ti, is_main = seq[n]
            xb = xs[n % 2]
            full = is_main or (ti == NT - 1)
            yield from gen_norm(xb, a1, shift1, hT)
            cur["n"] = n
            dbg("hT", hT.t[:], hT.r)
            yield from gen_proj_qkv(full)
            cur["n"] = n
            dbg("kn", kn.t[:], kn.r); dbg("vT", vT.t[:], vT.r); dbg("qn", qn.t[:], qn.r)
            stage_gates(absb, not is_main, eGlS[n % 2])
            for nm in ["g", "beta", "cP", "skbg", "skd", "cA"]:
                dbg("gt_" + nm, gt[nm].t[:], gt[nm].r)
            dbg("gsb", gsb.t[:], gsb.r)
            yield

        def gen_halo_mask():
            ts(halo.t[:], halo.t[:], vecs.t[:, V_PM:V_PM + 1], ALU.mult, [halo.r, vecs.r], [halo.r])
            yield

        def chain(*gens):
            for g_ in gens:
                yield from g_

        load_x(D[seq[0][0]], seq[0][1], 0)
        for _ in gen_pre(0):
            pass
        for n, (src, ti, is_main) in enumerate(seq):
            xb = xs[n % 2]
            cur["n"] = n
            has_next = n + 1 < len(seq)
            if has_next:
                load_x(D[seq[n + 1][0]], seq[n + 1][1], (n + 1) % 2)
            full = is_main or (ti == NT - 1)
            stage_mid(is_main)
            nxt = gen_pre(n + 1) if has_next else gen_empty()
            if is_main:
                merge(gen_states(True, eGlS[n % 2]), gen_proj_rest(), 1, 2)
                cur["n"] = n
                dbg("zs", zs.t[:], zs.r); dbg("ysc", yT.t[:], yT.r)
                dbg("S", S.t[:], S.r)
                dbg("Ot", Ot.t[:], Ot.r)
                stage_gdn_out()
                dbg("yT", yT.t[:], yT.r)
                stage_out(xb)
                dbg("x1", xb.t[:], xb.r)
                for _ in gen_norm(xb, a2, shift2, h2T):
                    pass
                merge(gen_ffn(xb), nxt, 1, 1)
                cur["n"] = n
                dbg("x2", xb.t[:], xb.r)
                stage_final(xb, ti)
            else:
                if full:
                    others = chain(gen_proj_rest(), gen_halo_mask(), nxt)
                else:
                    others = nxt
                merge(gen_states(False, eGlS[n % 2]), others, 1, 3)
        SP.prog.append(("wait", out_ds.sem, out_ds.count))
        if dbg_ds.count:
            SP.prog.append(("wait", dbg_ds.sem, dbg_ds.count))
        fw.finish()
    return nc


def _piece(W, c0, c1, nk=8):
    w = np.ascontiguousarray(W[:, c0:c1])
    n = c1 - c0
    return np.ascontiguousarray(w.reshape(nk, 128, n).transpose(1, 0, 2)).reshape(128, nk * n)


def _pad(a, n):
    out = np.zeros((a.shape[0], n), np.float32)
    out[:, :a.shape[1]] = a
    return out


def _fm(v):
    return np.ascontiguousarray(np.asarray(v, np.float32).reshape(-1, 128).T)


_CACHE = {}


def kernel(x, c, w_ada, b_ada, norm1_w, w_in, conv_qkv_w, A_log, dt_bias, gdn_norm_w,
           conv_short_w, w_out, norm2_w, w_gate, w_up, w_down, norm_f_w):
    f = lambda a: np.asarray(a, np.float32)
    x, c = f(x), f(c)
    w_ada, b_ada, w_in, w_out = f(w_ada)[0], f(b_ada)[0], f(w_in)[0], f(w_out)[0]
    w_gate, w_up, w_down = f(w_gate)[0], f(w_up)[0], f(w_down)[0]
    idx = np.arange(128)
    same = (idx[:, None] // 64) == (idx[None, :] // 64)
    idf = np.eye(128, dtype=np.float32)
    tri = (same & (idx[:, None] <= idx[None, :])).astype(np.float32)
    blk = same.astype(np.float32)
    indA = np.repeat((idx < 64).astype(np.float32)[:, None], 128, 1)
    indB = np.repeat((idx >= 64).astype(np.float32)[:, None], 128, 1)
    mnegs = np.where(same & (idx[:, None] > idx[None, :]), 0.0, NEG).astype(np.float32)
    mnegi = np.where(same & (idx[None, :] >= idx[:, None]), 0.0, NEG).astype(np.float32)
    consts = np.concatenate([idf, tri, blk, indA, indB, mnegs, mnegi], 1)
    wada = np.concatenate([_piece(w_ada, p * 512, (p + 1) * 512) for p in range(12)], 0)
    cols = [(0, 512), (512, 1024), (1024, 1536), (1536, 2048), (2568, 3080), (3080, 3592), (2056, 2568)]
    win = np.concatenate([_piece(w_in, a, b) for a, b in cols], 0)
    wab = _piece(w_in, 2048, 2056)
    wout = np.concatenate([_piece(w_out, p * 512, (p + 1) * 512) for p in range(2)], 0)
    gcols = [(i * 512, min((i + 1) * 512, 2816)) for i in range(6)]
    wg = np.concatenate([_pad(_piece(w_gate, a, b), 4096) for a, b in gcols], 0)
    wu = np.concatenate([_pad(_piece(w_up, a, b), 4096) for a, b in gcols], 0)
    wd = np.concatenate([_piece(w_down, d * 128, (d + 1) * 128, nk=22) for d in range(8)], 0)
    cw = f(conv_qkv_w)[0]
    cs = f(conv_short_w)[0]
    cwT = np.ascontiguousarray(cw.reshape(4, 12, 128).transpose(2, 1, 0)).reshape(128, 48)
    csT = np.ascontiguousarray(cs.reshape(3, 4, 128).transpose(2, 1, 0)).reshape(128, 12)
    in_maps = []
    for r in range(8):
        b, hf = r // 2, r % 2
        vecs = np.zeros((128, NV), np.float32)
        vecs[:, V_C:V_C + 8] = _fm(c[b])
        vecs[:, V_BADA:V_BADA + 48] = _fm(b_ada)
        vecs[:, V_N1:V_N1 + 8] = _fm(f(norm1_w)[0])
        vecs[:, V_N2:V_N2 + 8] = _fm(f(norm2_w)[0])
        vecs[:, V_NF:V_NF + 8] = _fm(f(norm_f_w))
        vecs[:, V_CW:V_CW + 48] = cwT
        vecs[:, V_CS:V_CS + 12] = csT
        vecs[:, V_ALOG:V_ALOG + 4] = f(A_log)[0][None, :]
        vecs[:, V_DTB:V_DTB + 4] = f(dt_bias)[0][None, :]
        vecs[:, V_GNW] = f(gdn_norm_w)[0]
        vecs[:, V_PM] = float(hf)
        in_maps.append({
            "xm": np.ascontiguousarray(x[b, hf * 4096:(hf + 1) * 4096].T),
            "xp": np.ascontiguousarray(x[b, 0:4096].T),
            "consts": consts, "vecs": vecs, "wada": wada, "win": win, "wab": wab, "wout": wout,
            "wg": wg, "wu": wu, "wd": wd,
        })
    if "nc" not in _CACHE:
        _CACHE["nc"] = build_program()
    if DBG["on"]:
        res = run_bass_kernel_spmd(_CACHE["nc"], in_maps[:DBG.get("ncores", 8)], core_ids=list(range(DBG.get("ncores", 8))))
        DBG["res"] = res.results
        return None
    res = run_bass_kernel_spmd(_CACHE["nc"], in_maps, core_ids=list(range(8)))
    out = np.empty((4, 8192, 1024), np.float32)
    for r in range(8):
        b, hf = r // 2, r % 2
        out[b, hf * 4096:(hf + 1) * 4096] = res.results[r]["out"].T
    return out
```

```python
import numpy as np
from contextlib import ExitStack
import concourse.bass as bass
import concourse.mybir as mybir
from concourse.bass_utils import run_bass_kernel_spmd

F32 = mybir.dt.float32
BF16 = mybir.dt.bfloat16
AF = mybir.ActivationFunctionType
ALU = mybir.AluOpType

TT = 512
NT = 8
NEG = -30000.0
RMS_EPS = 1e-6
L2_EPS = 1e-6

V_C, V_BADA, V_N1, V_N2, V_NF, V_CW, V_CS, V_ALOG, V_DTB, V_GNW, V_PM = 0, 8, 56, 64, 72, 80, 128, 140, 144, 148, 149
NV = 150


class Tk:
    __slots__ = ("sem", "val", "key")

    def __init__(self, sem, val, key):
        self.sem, self.val, self.key = sem, val, key


class Res:
    __slots__ = ("w", "r")

    def __init__(self):
        self.w = None
        self.r = {}


class Eng:
    def __init__(self, name, sem):
        self.name, self.sem = name, sem
        self.count = 0
        self.waited = {}
        self.prog = []


class DSem:
    def __init__(self, sem, key):
        self.sem, self.key, self.count = sem, key, 0


class FW:
    def __init__(self, nc, stack):
        self.nc, self.stack = nc, stack
        mk = lambda n: Eng(n, stack.enter_context(nc.semaphore("s_" + n)))
        self.PE, self.ACT, self.DVE, self.POOL, self.SP = mk("pe"), mk("act"), mk("dve"), mk("pool"), mk("sp")
        self.nds = 0

    def dsem(self):
        self.nds += 1
        return DSem(self.stack.enter_context(self.nc.semaphore("s_d%d" % self.nds)), "d%d" % self.nds)

    def sb(self, name, shape, dt):
        return self.stack.enter_context(self.nc.sbuf_tensor(name, list(shape), dt))

    def ps(self, name, shape, dt):
        return self.stack.enter_context(self.nc.psum_tensor(name, list(shape), dt))

    def _wait(self, E, t):
        if t.key == E.name and E.name == "pe":
            return
        if E.waited.get(t.key, 0) >= t.val:
            return
        E.prog.append(("wait", t.sem, t.val))
        E.waited[t.key] = t.val

    @staticmethod
    def _flat(lst):
        out = []
        for b in lst:
            if isinstance(b, (list, tuple)):
                out.extend(FW._flat(b))
            elif b is not None:
                out.append(b)
        return out

    def _deps(self, E, reads, writes):
        reads, writes = self._flat(reads), self._flat(writes)
        for b in reads:
            if b.w is not None:
                self._wait(E, b.w)
        for b in writes:
            if b.w is not None and b.w.key != E.name:
                self._wait(E, b.w)
            for t in b.r.values():
                if t.key != E.name:
                    self._wait(E, t)

    def _mark(self, tk, reads, writes):
        reads, writes = self._flat(reads), self._flat(writes)
        for b in reads:
            o = b.r.get(tk.key)
            if o is None or o.val < tk.val:
                b.r[tk.key] = tk
        for b in writes:
            b.w = tk
            b.r = {}

    def op(self, E, fn, reads=(), writes=()):
        self._deps(E, reads, writes)
        E.count += 1
        tk = Tk(E.sem, E.count, E.name)
        E.prog.append(("op", fn))
        self._mark(tk, reads, writes)
        return tk

    def dma(self, E, ds, fn, reads=(), writes=()):
        self._deps(E, reads, writes)
        ds.count += 16
        tk = Tk(ds.sem, ds.count, ds.key)
        E.prog.append(("dma", fn, ds.sem))
        self._mark(tk, reads, writes)
        return tk

    def finish(self):
        engs = [(self.PE, "tensor"), (self.ACT, "scalar"), (self.DVE, "vector"), (self.POOL, "gpsimd"), (self.SP, "sync")]
        with self.nc.Block() as block:
            for E, attr in engs:
                def body(eng, E=E):
                    for it in E.prog:
                        if it[0] == "wait":
                            eng.wait_ge(it[1], it[2])
                        elif it[0] == "op":
                            it[1](eng).then_inc(E.sem, 1)
                        else:
                            it[1](eng).then_inc(it[2], 16)
                getattr(block, attr)(body)


class B:
    def __init__(self, t, nres=1):
        self.t = t
        if nres == 1:
            self.r = Res()
        else:
            self.rh = [Res() for _ in range(nres)]
            self.r = tuple(self.rh)

    def __getitem__(self, k):
        return self.t[k]


DBG = {"on": False, "seq": None, "names": []}


def build_program():
    nc = bass.Bass("TRN2", target_bir_lowering=False)
    D = {}
    DBG["names"] = []

    def din(name, shape):
        D[name] = nc.dram_tensor(name, list(shape), F32, kind="ExternalInput").ap()

    din("xm", [1024, 4096]); din("xp", [1024, 4096])
    din("consts", [128, 7 * 128]); din("vecs", [128, NV])
    din("wada", [12 * 128, 4096]); din("win", [7 * 128, 4096]); din("wab", [128, 64])
    din("wout", [2 * 128, 4096]); din("wg", [6 * 128, 4096]); din("wu", [6 * 128, 4096]); din("wd", [8 * 128, 2816])
    outd = nc.dram_tensor("out", [1024, 4096], F32, kind="ExternalOutput").ap()

    st = ExitStack()
    with st:
        fw = FW(nc, st)
        PE, ACT, DVE, POOL, SP = fw.PE, fw.ACT, fw.DVE, fw.POOL, fw.SP

        def sb(name, shape, dt=F32, nres=1):
            return B(fw.sb("sb_" + name, shape, dt), nres)

        def mm(out, lhsT, rhs, start, stop, rd, wr):
            fw.op(PE, lambda e: e.matmul(out, lhsT=lhsT, rhs=rhs, start=start, stop=stop), rd, wr)

        def tr(out, in_, ident, rd, wr):
            fw.op(PE, lambda e: e.transpose(out, in_, ident), rd, wr)

        def act(out, in_, func, rd, wr, bias=None, scale=None):
            kw = {}
            if bias is not None:
                kw["bias"] = bias
            if scale is not None:
                kw["scale"] = scale
            fw.op(ACT, lambda e: e.activation(out=out, in_=in_, func=func, **kw), rd, wr)

        def tt(out, in0, in1, op, rd, wr, E=None):
            fw.op(E or DVE, lambda e: e.tensor_tensor(out=out, in0=in0, in1=in1, op=op), rd, wr)

        def ts(out, in0, s1, op0, rd, wr, s2=None, op1=None, E=None):
            if op1 is None:
                fw.op(E or DVE, lambda e: e.tensor_scalar(out=out, in0=in0, scalar1=s1, scalar2=None, op0=op0), rd, wr)
            else:
                fw.op(E or DVE, lambda e: e.tensor_scalar(out=out, in0=in0, scalar1=s1, scalar2=s2, op0=op0, op1=op1), rd, wr)

        def stt(out, in0, scalar, in1, op0, op1, rd, wr):
            fw.op(DVE, lambda e: e.scalar_tensor_tensor(out=out, in0=in0, scalar=scalar, in1=in1, op0=op0, op1=op1), rd, wr)

        def cp(out, in_, rd, wr, E=None):
            fw.op(E or DVE, lambda e: e.tensor_copy(out=out, in_=in_), rd, wr)

        def recip(out, in_, rd, wr):
            fw.op(DVE, lambda e: e.reciprocal(out=out, in_=in_), rd, wr)

        def h4(ap):
            return ap.rearrange("p (h c) -> p h c", h=4)

        consts = sb("consts", [128, 7 * 128])
        idf = consts.t[:, 0:128]; tri = consts.t[:, 128:256]; blk = consts.t[:, 256:384]
        indA = consts.t[:, 384:512]; indB = consts.t[:, 512:640]; mnegs = consts.t[:, 640:768]; mnegi = consts.t[:, 768:896]
        vecs = sb("vecs", [128, NV])
        idb = sb("idb", [128, 128], BF16)
        ones = sb("ones", [128, 128], BF16)
        mod = sb("mod", [128, 64])
        cact = sb("cact", [128, 8], BF16)
        ctmp = sb("ctmp", [128, 16])
        negA = sb("negA", [128, 4])
        wab = sb("wab", [128, 64], BF16)
        xs = [sb("xs%d" % i, [128, 8, TT]) for i in range(2)]
        xds = [fw.dsem() for _ in range(2)]
        hT = sb("hT", [128, 8, TT], BF16)
        h2T = sb("h2T", [128, 8, TT], BF16)
        yT = sb("yT", [128, 8, TT], BF16)
        sqb = [sb("sqb%d" % i, [128, TT], BF16) for i in range(2)]
        rstd = sb("rstd", [128, TT]); lnv = rstd
        tmp = [sb("tmp%d" % i, [128, TT]) for i in range(1)]
        NSLOT = 6
        wr_ = [sb("wr%d" % i, [128, 2048], BF16) for i in range(NSLOT)]
        for w_ in wr_:
            w_.pinned = False
        wds = [fw.dsem() for _ in range(NSLOT)]
        qn = sb("qn", [128, 4, TT], BF16); kn = sb("kn", [128, 4, TT], BF16); vT = sb("vT", [128, 4, TT], BF16)
        zs = sb("zs", [128, 4, TT], BF16)
        raw = [sb("raw%d" % i, [128, TT]) for i in range(4)]
        pin = [sb("pin%d" % i, [128, TT + 3]) for i in range(1)]
        acc = [sb("acc%d" % i, [128, TT]) for i in range(2)]
        halo = sb("halo", [128, 16, 3])
        sccs = sb("sccs", [128, 4, TT], BF16); cvo = sb("cvo", [128, 4, TT], BF16)
        sgb = [sb("sgb%d" % i, [128, TT]) for i in range(2)]
        arena = fw.sb("sb_arena", [128, 6144], F32)
        pages = [Res() for _ in range(24)]

        class V:
            def __init__(self, t, r):
                self.t, self.r = t, r
        arena_bf = arena[:].bitcast(BF16)
        actv = [V(arena_bf[:, fc * 512:(fc + 1) * 512], pages[fc]) for fc in range(22)]
        Ot = sb("Ot", [128, 4, TT], BF16)
        S = sb("S", [128, 4, 128], nres=4); Sb = sb("Sb", [128, 4, 128], BF16)
        gt = {n: sb("g_" + n, [128, 16]) for n in ["xa", "ax", "e1", "l1", "g", "ng", "eb", "opb", "beta", "lnopb", "cP", "cA", "skbg", "skd", "t1"]}
        gsb = sb("gsb", [128, 64]); eGlS = [sb("eGl%d" % i, [128, 32]) for i in range(2)]; absb = sb("absb", [128, 32])
        nrepS = [sb("nrep%d" % i, [128, 4, 128], nres=4) for i in range(2)]
        EsS = [sb("Es%d" % i, [128, 4, 128], nres=4) for i in range(2)]
        PmS = [V(arena[:, c * 1536:c * 1536 + 512].rearrange("p (h c) -> p h c", h=4), tuple(pages[6 * c:6 * c + 2])) for c in range(4)]
        PRS = [V(arena[:, c * 1536 + 512:(c + 1) * 1536].rearrange("p (h t c) -> p h t c", h=4, t=2), tuple(pages[6 * c + 2:6 * c + 6])) for c in range(4)]
        for pr_ in PRS:
            pr_.ht = {(hp, t_): Res() for hp in range(2) for t_ in range(2)}
        TTbS = [sb("TTb%d" % i, [128, 4, 128], BF16) for i in range(4)]
        KbgS = [sb("Kbg%d" % i, [128, 4, 128], BF16) for i in range(4)]
        KdS = [sb("Kd%d" % i, [128, 4, 128], BF16) for i in range(4)]
        VbS = [sb("Vb%d" % i, [128, 4, 128], BF16) for i in range(4)]
        attnTS = [sb("attnT%d" % i, [128, 4, 128], BF16) for i in range(4)]
        QdS = [sb("Qd%d" % i, [128, 4, 128], BF16) for i in range(4)]
        nWtS = [sb("nWt%d" % i, [128, 4, 128], BF16) for i in range(4)]
        VnS = [sb("Vn%d" % i, [128, 4, 128], BF16) for i in range(2)]
        banks = [B(fw.ps("pb%d" % i, [128, 512], F32)) for i in range(8)]
        bstate = {"i": 0, "w": 0, "x": 0, "s": 0, "t": 0, "pin": 0, "acc": 0, "sg": 0}

        def nb():
            b = banks[bstate["i"] % 8]
            bstate["i"] += 1
            return b

        def rot(lst, key):
            b = lst[bstate[key] % len(lst)]
            bstate[key] += 1
            return b

        misc_ds = fw.dsem()
        out_ds = fw.dsem()
        dbg_ds = fw.dsem()
        cur = {"n": -1}

        def dbg(tag, ap, res, cond=True):
            if not (DBG["on"] and cond):
                return
            name = "dbg_%s_%d" % (tag, cur["n"])
            if name in DBG["names"]:
                return
            DBG["names"].append(name)
            shp = list(ap.shape)
            dt_ = ap.dtype
            dr = nc.dram_tensor(name, shp, dt_, kind="ExternalOutput").ap()
            fw.dma(SP, dbg_ds, lambda e: e.dma_start(out=dr, in_=ap), [res], [])

        def piece(src_ap, nj, n):
            for k in range(NSLOT):
                i = (bstate["w"] + k) % NSLOT
                if not wr_[i].pinned:
                    break
            else:
                raise RuntimeError("all weight slots pinned")
            bstate["w"] = i + 1
            slot = wr_[i]
            slot.pinned = True
            dst = slot.t[:, 0:nj * n].rearrange("p (j n) -> p j n", j=nj)
            fw.dma(POOL, wds[i], lambda e: e.dma_start(out=dst, in_=src_ap, max_dma_last_dim=4096), [], [slot.r])
            slot.wv = dst
            return slot

        def piece_rows(dname, row0, nj, ncols, c0, n):
            src = D[dname][row0:row0 + 128, 0:nj * ncols].rearrange("p (j n) -> p j n", j=nj)[:, :, c0:c0 + n]
            return piece(src, nj, n)

        fw.dma(SP, misc_ds, lambda e: e.dma_start(out=consts.t[:], in_=D["consts"]), [], [consts.r])
        misc_ds2 = fw.dsem()
        fw.dma(SP, misc_ds2, lambda e: e.dma_start(out=vecs.t[:], in_=D["vecs"]), [], [vecs.r])
        wabds = fw.dsem()
        fw.dma(POOL, wabds, lambda e: e.dma_start(out=wab.t[:], in_=D["wab"]), [], [wab.r])
        cp(idb.t[:], idf, [consts.r], [idb.r])
        fw.op(DVE, lambda e: e.memset(ones.t[:], 1.0), [], [ones.r])
        fw.op(DVE, lambda e: e.memset(halo.t[:], 0.0), [], [halo.r])
        fw.op(DVE, lambda e: e.memset(S.t[:], 0.0), [], [S.r])
        fw.op(DVE, lambda e: e.memset(Sb.t[:], 0.0), [], [Sb.r])
        for vn_ in VnS:
            fw.op(DVE, lambda e, vn_=vn_: e.memset(vn_.t[:], 0.0), [], [vn_.r])
        act(ctmp.t[:, 0:8], vecs.t[:, V_C:V_C + 8], AF.Exp, [vecs.r], [ctmp.r], scale=-1.0)
        ts(ctmp.t[:, 0:8], ctmp.t[:, 0:8], 1.0, ALU.add, [ctmp.r], [ctmp.r])
        recip(ctmp.t[:, 0:8], ctmp.t[:, 0:8], [ctmp.r], [ctmp.r])
        tt(cact.t[:], ctmp.t[:, 0:8], vecs.t[:, V_C:V_C + 8], ALU.mult, [ctmp.r, vecs.r], [cact.r])
        act(negA.t[:], vecs.t[:, V_ALOG:V_ALOG + 4], AF.Exp, [vecs.r], [negA.r])
        ts(negA.t[:], negA.t[:], -1.0, ALU.mult, [negA.r], [negA.r])
        mb = nb()
        for pc in range(12):
            for hf_ in range(2):
                slot = piece_rows("wada", pc * 128, 8, 512, hf_ * 256, 256)
                wv = slot.wv
                for o2 in range(2):
                    col = pc * 4 + hf_ * 2 + o2
                    for j in range(8):
                        mm(mb.t[:, col:col + 1], wv[:, j, o2 * 128:(o2 + 1) * 128], cact.t[:, j:j + 1], j == 0, j == 7,
                           [slot.r, cact.r], [mb.r])
                slot.pinned = False
        tt(mod.t[:, 0:48], mb.t[:, 0:48], vecs.t[:, V_BADA:V_BADA + 48], ALU.add, [mb.r, vecs.r], [mod.r])
        dbg("vecs", vecs.t[:], vecs.r)
        stt(mod.t[:, 48:56], mod.t[:, 8:16], 1.0, vecs.t[:, V_N1:V_N1 + 8], ALU.add, ALU.mult, [mod.r, vecs.r], [mod.r])
        stt(mod.t[:, 56:64], mod.t[:, 32:40], 1.0, vecs.t[:, V_N2:V_N2 + 8], ALU.add, ALU.mult, [mod.r, vecs.r], [mod.r])
        shift1, gate1, shift2, gate2 = mod.t[:, 0:8], mod.t[:, 16:24], mod.t[:, 24:32], mod.t[:, 40:48]
        a1, a2 = mod.t[:, 48:56], mod.t[:, 56:64]

        def load_x(src, ti, slot_i):
            xb = xs[slot_i]
            srcv = src.rearrange("(j p) t -> p j t", p=128)[:, :, ti * TT:(ti + 1) * TT]
            fw.dma(SP, xds[slot_i], lambda e: e.dma_start(out=xb.t[:], in_=srcv), [], [xb.r])

        def rms_rstd(xb, nj, width_scale, eps, extra_bias=None):
            bk = nb()
            for j, (ap, r) in enumerate(xb):
                sq = rot(sqb, "s")
                act(sq.t[:], ap, AF.Square, [r], [sq.r])
                mm(bk.t[:], ones.t[:], sq.t[:], j == 0, j == nj - 1, [ones.r, sq.r], [bk.r])
            act(lnv.t[:], bk.t[:], AF.Ln, [bk.r], [lnv.r], bias=eps, scale=width_scale)
            if extra_bias is None:
                act(rstd.t[:], lnv.t[:], AF.Exp, [lnv.r], [rstd.r], scale=-0.5)
            else:
                act(rstd.t[:], lnv.t[:], AF.Exp, [lnv.r], [rstd.r], scale=-0.5, bias=extra_bias)

        def gen_norm(xb, a_, sh_, dst):
            rms_rstd([(xb.t[:, j, :], xb.r) for j in range(8)], 8, 1.0 / 1024.0, RMS_EPS)
            yield
            for j in range(8):
                t_ = rot(tmp, "t")
                stt(t_.t[:], xb.t[:, j, :], a_[:, j:j + 1], rstd.t[:], ALU.mult, ALU.mult, [xb.r, mod.r, rstd.r], [t_.r])
                act(dst.t[:, j, :], t_.t[:], AF.Identity, [t_.r, mod.r], [dst.r], bias=sh_[:, j:j + 1])
                yield

        def proj_chunk(wv, cc, slot, rhs_b, nk):
            bk = nb()
            for j in range(nk):
                mm(bk.t[:], wv[:, j, cc * 128:(cc + 1) * 128], rhs_b.t[:, j, :], j == 0, j == nk - 1, [slot.r, rhs_b.r], [bk.r])
            return bk

        def conv_chunk(src_ap, src_r, ch, ntap, wcol0, out_ap, out_r, from_psum_copy=True):
            p_ = rot(pin, "pin")
            hl = ntap - 1
            if from_psum_copy:
                act(p_.t[:, hl:hl + TT], src_ap, AF.Copy, src_r, [p_.r])
            else:
                src_ap(p_.t[:, hl:hl + TT], p_.r)
            cp(p_.t[:, 0:hl], halo.t[:, ch, 0:hl], [halo.r], [p_.r])
            cp(halo.t[:, ch, 0:hl], p_.t[:, TT:TT + hl], [p_.r], [halo.r])
            ts(out_ap, p_.t[:, 0:TT], vecs.t[:, wcol0:wcol0 + 1], ALU.mult, [p_.r, vecs.r], [out_r])
            for k in range(1, ntap):
                stt(out_ap, p_.t[:, k:k + TT], vecs.t[:, wcol0 + k:wcol0 + k + 1], out_ap, ALU.mult, ALU.add,
                    [p_.r, vecs.r, out_r], [out_r])

        def wpiece(i, hf_):
            slot = piece_rows("win", i * 128, 8, 512, hf_ * 256, 256)
            return slot, slot.wv

        def gen_proj_qkv(full):
            for pi in ([0, 1, 2] if full else [1, 2]):
                for cc in range(4):
                    if cc % 2 == 0:
                        slot, wv = wpiece(pi, cc // 2)
                    bk = proj_chunk(wv, cc % 2, slot, hT, 8)
                    if cc % 2 == 1:
                        slot.pinned = False
                    ch = pi * 4 + cc
                    a_ = rot(acc, "acc")
                    conv_chunk(bk.t[:], [bk.r], ch, 4, V_CW + ch * 4, a_.t[:], a_.r)
                    if pi == 2:
                        act(vT.t[:, cc, :], a_.t[:], AF.Silu, [a_.r], [vT.r])
                    else:
                        act(raw[cc].t[:], a_.t[:], AF.Silu, [a_.r], [raw[cc].r])
                    yield
                if pi != 2:
                    dst = qn if pi == 0 else kn
                    for cc in range(4):
                        rms_rstd([(raw[cc].t[:], raw[cc].r)], 1, 1.0, L2_EPS, extra_bias=(-0.5 * np.log(128.0) if pi == 0 else None))
                        tt(dst.t[:, cc, :], raw[cc].t[:], rstd.t[:], ALU.mult, [raw[cc].r, rstd.r], [dst.r])
                        yield
            abk = nb()
            wabv = wab.t[:].rearrange("p (j n) -> p j n", j=8)
            for ci in range(4):
                for j in range(8):
                    mm(abk.t[:, ci * 8:(ci + 1) * 8], hT.t[:, j, ci * 128:(ci + 1) * 128], wabv[:, j, :], j == 0, j == 7,
                       [hT.r, wab.r], [abk.r])
            cp(absb.t[:], abk.t[:, 0:32], [abk.r], [absb.r])
            yield

        def gen_proj_rest():
            for cc in range(4):
                if cc % 2 == 0:
                    slot, wv = wpiece(3, cc // 2)
                bk = proj_chunk(wv, cc % 2, slot, hT, 8)
                if cc % 2 == 1:
                    slot.pinned = False
                act(zs.t[:, cc, :], bk.t[:], AF.Silu, [bk.r], [zs.r])
                yield
            for cc in range(4):
                if cc % 2 == 0:
                    slot, wv = wpiece(4, cc // 2)
                bk = proj_chunk(wv, cc % 2, slot, hT, 8)
                if cc % 2 == 1:
                    slot.pinned = False
                act(sccs.t[:, cc, :], bk.t[:], AF.Copy, [bk.r], [sccs.r])
                yield
            for cc in range(4):
                if cc % 2 == 0:
                    slot, wv = wpiece(5, cc // 2)
                bk = proj_chunk(wv, cc % 2, slot, hT, 8)
                if cc % 2 == 1:
                    slot.pinned = False

                def prod(dst_ap, dst_r, bk=bk, cc=cc):
                    tt(dst_ap, bk.t[:], sccs.t[:, cc, :], ALU.mult, [bk.r, sccs.r], [dst_r])
                conv_chunk(prod, None, 12 + cc, 3, V_CS + cc * 3, cvo.t[:, cc, :], cvo.r, from_psum_copy=False)
                yield
            for cc in range(4):
                if cc % 2 == 0:
                    slot, wv = wpiece(6, cc // 2)
                bk = proj_chunk(wv, cc % 2, slot, hT, 8)
                if cc % 2 == 1:
                    slot.pinned = False
                tt(yT.t[:, 4 + cc, :], bk.t[:], cvo.t[:, cc, :], ALU.mult, [bk.r, cvo.r], [yT.r])
                yield

        def g16(n):
            return gt[n].t[:].rearrange("p (c h) -> p c h", c=4)

        def stage_gates(abk, masked, eGl):
            abv = abk.t[:].rearrange("p (c t h) -> p c t h", c=4, t=2)
            a_ap, b_ap = abv[:, :, 0, :], abv[:, :, 1, :]
            G = lambda n: gt[n]
            dtb = vecs.t[:, V_DTB:V_DTB + 4].unsqueeze(1).to_broadcast([128, 4, 4])
            nAb = negA.t[:].unsqueeze(1).to_broadcast([128, 4, 4])
            tt(g16("xa"), a_ap, dtb, ALU.add, [abk.r, vecs.r], [G("xa").r])
            act(G("ax").t[:], G("xa").t[:], AF.Abs, [G("xa").r], [G("ax").r])
            act(G("e1").t[:], G("ax").t[:], AF.Exp, [G("ax").r], [G("e1").r], scale=-1.0)
            act(G("l1").t[:], G("e1").t[:], AF.Ln, [G("e1").r], [G("l1").r], bias=1.0)
            stt(G("t1").t[:], G("xa").t[:], 0.0, G("l1").t[:], ALU.max, ALU.add, [G("xa").r, G("l1").r], [G("t1").r])
            tt(g16("g"), g16("t1"), nAb, ALU.mult, [G("t1").r, negA.r], [G("g").r])
            ts(G("ng").t[:], G("g").t[:], -1.0, ALU.mult, [G("g").r], [G("ng").r])
            act(g16("eb"), b_ap, AF.Exp, [abk.r], [G("eb").r], scale=-1.0)
            ts(G("opb").t[:], G("eb").t[:], 1.0, ALU.add, [G("eb").r], [G("opb").r])
            recip(G("beta").t[:], G("opb").t[:], [G("opb").r], [G("beta").r])
            act(G("lnopb").t[:], G("opb").t[:], AF.Ln, [G("opb").r], [G("lnopb").r])
            gb = nb()
            gsrc = G("g")
            mm(gb.t[:, 0:16], tri, gsrc.t[:], True, True, [consts.r, gsrc.r], [gb.r])
            mm(gb.t[:, 16:32], blk, gsrc.t[:], True, True, [consts.r, gsrc.r], [gb.r])
            mm(gb.t[:, 32:48], indA, gsrc.t[:], True, True, [consts.r, gsrc.r], [gb.r])
            mm(gb.t[:, 48:64], indB, gsrc.t[:], True, True, [consts.r, gsrc.r], [gb.r])
            cp(gsb.t[:], gb.t[:, 0:64], [gb.r], [gsb.r])
            Gc, Glo = gsb.t[:, 0:16], gsb.t[:, 16:32]
            tt(G("cP").t[:], Gc, G("lnopb").t[:], ALU.subtract, [gsb.r, G("lnopb").r], [G("cP").r])
            act(G("skbg").t[:], G("cP").t[:], AF.Exp, [G("cP").r], [G("skbg").r])
            ts(G("cA").t[:], Gc, -1.0, ALU.mult, [gsb.r], [G("cA").r])
            tt(G("skd").t[:], Glo, Gc, ALU.subtract, [gsb.r], [G("skd").r])
            act(G("skd").t[:], G("skd").t[:], AF.Exp, [G("skd").r], [G("skd").r])
            act(eGl.t[:], gsb.t[:, 32:64], AF.Exp, [gsb.r], [eGl.r])
            if masked:
                pm = vecs.t[:, V_PM:V_PM + 1]
                ts(G("skbg").t[:], G("skbg").t[:], pm, ALU.mult, [G("skbg").r, vecs.r], [G("skbg").r])
                ts(G("beta").t[:], G("beta").t[:], pm, ALU.mult, [G("beta").r, vecs.r], [G("beta").r])

        def bc4(ap_ci):
            return ap_ci.unsqueeze(2).to_broadcast([128, 4, 128])

        def bcm(mask_ap):
            return mask_ap.unsqueeze(1).to_broadcast([128, 4, 128])

        def gdn_front(ci, full):
            cs = slice(ci * 128, (ci + 1) * 128)
            G = lambda n: gt[n]
            nrep = nrepS[ci % 2]; Es = EsS[ci % 2]; Ea = nrep; eGb = Es
            Pm, PR = PmS[ci], PRS[ci]
            Kbg, Kd, Vb, attnT, Qd = KbgS[ci], KdS[ci], VbS[ci], attnTS[ci], QdS[ci]
            cp(nrep.t[:], bc4(g16("ng")[:, ci, :]), [G("ng").r], [nrep.r])
            bA = nb()
            for h in range(4):
                mm(bA.t[:, h * 128:(h + 1) * 128], nrep.t[:, h, :], tri, True, True, [nrep.rh[h], consts.r], [bA.r])
            bB = nb()
            for h in range(4):
                mm(bB.t[:, h * 128:(h + 1) * 128], kn.t[:, h, cs], kn.t[:, h, cs], True, True, [kn.r], [bB.r])
            tt(Es.t[:], h4(bA.t[:]), bcm(mnegs), ALU.add, [bA.r, consts.r], [Es.r])
            for h in range(4):
                act(Es.t[:, h, :], Es.t[:, h, :], AF.Exp, [Es.rh[h], G("cP").r], [Es.rh[h]], bias=g16("cP")[:, ci, h:h + 1])
            stt(Pm.t, h4(bB.t[:]), -1.0, Es.t[:], ALU.mult, ALU.mult, [bB.r, Es.r], [Pm.r])
            dbg("P0", Pm.t, Pm.r, ci == 0)
            bG = nb()
            bGb = bG.t[:].bitcast(BF16)
            for h in range(4):
                tr(bGb[:, h * 128:(h + 1) * 128], kn.t[:, h, cs], idb.t[:], [kn.r, idb.r], [bG.r])
            for h in range(4):
                tr(bGb[:, 512 + h * 128:512 + (h + 1) * 128], vT.t[:, h, cs], idb.t[:], [vT.r, idb.r], [bG.r])
            ktok = bGb[:, 0:512].rearrange("p (h c) -> p h c", h=4)
            vtok = bGb[:, 512:1024].rearrange("p (h c) -> p h c", h=4)
            tt(Kbg.t[:], ktok, bc4(g16("skbg")[:, ci, :]), ALU.mult, [bG.r, G("skbg").r], [Kbg.r])
            tt(Kd.t[:], ktok, bc4(g16("skd")[:, ci, :]), ALU.mult, [bG.r, G("skd").r], [Kd.r])
            tt(Vb.t[:], vtok, bc4(g16("beta")[:, ci, :]), ALU.mult, [bG.r, G("beta").r], [Vb.r])
            if full:
                bH = nb()
                for h in range(4):
                    mm(bH.t[:, h * 128:(h + 1) * 128], kn.t[:, h, cs], qn.t[:, h, cs], True, True, [kn.r, qn.r], [bH.r])
                tt(Ea.t[:], h4(bA.t[:]), bcm(mnegi), ALU.subtract, [bA.r, consts.r], [Ea.r])
                for h in range(4):
                    act(Ea.t[:, h, :], Ea.t[:, h, :], AF.Exp, [Ea.rh[h], G("cA").r], [Ea.rh[h]], bias=g16("cA")[:, ci, h:h + 1], scale=-1.0)
                tt(attnT.t[:], h4(bH.t[:]), Ea.t[:], ALU.mult, [bH.r, Ea.r], [attnT.r])
                act(eGb.t[:], h4(bA.t[:]), AF.Exp, [bA.r], [eGb.r], scale=-1.0)
                tt(Qd.t[:], qn.t[:, :, cs], eGb.t[:], ALU.mult, [qn.r, eGb.r], [Qd.r])
            bC = nb()
            for h in range(4):
                tr(bC.t[:, h * 128:(h + 1) * 128], Pm.t[:, h, :], idf, [Pm.r, consts.r], [bC.r])
            act(PR.t[:, :, 0, :], h4(bC.t[:]), AF.Copy, [bC.r], [PR.r, PR.ht[(0, 0)], PR.ht[(1, 0)]])
            tt(PR.t[:, :, 1, :], h4(bC.t[:]), bcm(idf), ALU.add, [bC.r, consts.r], [PR.r, PR.ht[(0, 1)], PR.ht[(1, 1)]])

        def gdn_level(ci, lvl):
            Pm, PR, TTb = PmS[ci], PRS[ci], TTbS[ci]
            if lvl == 0:
                bD = nb(); bE = nb()
                for h in range(4):
                    mm(bD.t[:, h * 128:(h + 1) * 128], PR.t[:, h, 0, :], Pm.t[:, h, :], True, True, [PR.ht[(h // 2, 0)], Pm.r], [bD.r])
                for h in range(4):
                    mm(bE.t[:, h * 128:(h + 1) * 128], Pm.t[:, h, :], PR.t[:, h, 0, :], True, True, [PR.ht[(h // 2, 0)], Pm.r], [bE.r])
                act(Pm.t, h4(bD.t[:]), AF.Copy, [bD.r], [Pm.r])
                cp(PR.t[:, :, 0, :], h4(bE.t[:]), [bE.r], [PR.ht[(0, 0)], PR.ht[(1, 0)]])
            elif lvl in (1, 2, 3):
                bD = nb(); bE = nb(); bF = nb()
                for h in range(4):
                    mm(bD.t[:, h * 128:(h + 1) * 128], PR.t[:, h, 0, :], Pm.t[:, h, :], True, True, [PR.ht[(h // 2, 0)], Pm.r], [bD.r])
                for h in range(4):
                    bb = bE if h < 2 else bF
                    o = (h % 2) * 256
                    mm(bb.t[:, o:o + 256], Pm.t[:, h, :], PR.t[:, h, :, :].rearrange("p t c -> p (t c)"), True, True,
                       [PR.ht[(h // 2, 0)], PR.ht[(h // 2, 1)], Pm.r], [bb.r])
                act(Pm.t, h4(bD.t[:]), AF.Copy, [bD.r], [Pm.r])
                for hp, (bb, hs) in enumerate(((bE, slice(0, 2)), (bF, slice(2, 4)))):
                    v = bb.t[:].rearrange("p (h t c) -> p h t c", h=2, t=2)
                    tt(PR.t[:, hs, 1, :], v[:, :, 1, :], PR.t[:, hs, 1, :], ALU.add, [bb.r, PR.ht[(hp, 1)]], [PR.ht[(hp, 1)]])
                    act(PR.t[:, hs, 0, :], v[:, :, 0, :], AF.Copy, [bb.r], [PR.ht[(hp, 0)]])
            elif lvl == 4:
                bD = nb(); bE = nb()
                for h in range(4):
                    mm(bD.t[:, h * 128:(h + 1) * 128], PR.t[:, h, 0, :], Pm.t[:, h, :], True, True, [PR.ht[(h // 2, 0)], Pm.r], [bD.r])
                for h in range(4):
                    mm(bE.t[:, h * 128:(h + 1) * 128], Pm.t[:, h, :], PR.t[:, h, 1, :], True, True, [PR.ht[(h // 2, 1)], Pm.r], [bE.r])
                act(Pm.t, h4(bD.t[:]), AF.Copy, [bD.r], [Pm.r])
                tt(PR.t[:, :, 1, :], h4(bE.t[:]), PR.t[:, :, 1, :], ALU.add, [bE.r, PR.ht[(0, 1)], PR.ht[(1, 1)]], [PR.ht[(0, 1)], PR.ht[(1, 1)]])
            elif lvl == 5:
                bE = nb()
                for h in range(4):
                    mm(bE.t[:, h * 128:(h + 1) * 128], Pm.t[:, h, :], PR.t[:, h, 1, :], True, True, [PR.ht[(h // 2, 1)], Pm.r], [bE.r])
                tt(TTb.t[:], h4(bE.t[:]), PR.t[:, :, 1, :], ALU.add, [bE.r, PR.r, PR.ht[(0, 1)], PR.ht[(1, 1)]], [TTb.r])
            else:
                Kbg, nWt = KbgS[ci], nWtS[ci]
                dbg("TTb", TTb.t[:], TTb.r, ci == 0)
                bI = nb()
                for h in range(4):
                    mm(bI.t[:, h * 128:(h + 1) * 128], Kbg.t[:, h, :], TTb.t[:, h, :], True, True, [Kbg.r, TTb.r], [bI.r])
                act(nWt.t[:], h4(bI.t[:]), AF.Copy, [bI.r], [nWt.r], scale=-1.0)

        def gen_gdn_state(ci, full, eGl):
            cs = slice(ci * 128, (ci + 1) * 128)
            TTb, Kd, Vb, attnT, Qd, nWt = TTbS[ci], KdS[ci], VbS[ci], attnTS[ci], QdS[ci], nWtS[ci]
            bJ = nb()
            bK = nb() if full else None
            for s in range(2):
                rs = slice(s * 64, (s + 1) * 64)
                Vn = VnS[s]
                for h in range(4):
                    o = bJ.t[rs, h * 128:(h + 1) * 128]
                    mm(o, TTb.t[:, h, rs], Vb.t[:, h, :], True, False, [TTb.r, Vb.r], [bJ.r])
                    mm(o, nWt.t[:, h, rs], Sb.t[:, h, :], False, True, [nWt.r, Sb.r], [bJ.r])
                act(Vn.t[rs, :, :], h4(bJ.t[rs, :]), AF.Copy, [bJ.r], [Vn.r])
                if full:
                    for h in range(4):
                        o = bK.t[:, h * 128 + s * 64:h * 128 + (s + 1) * 64]
                        mm(o, Sb.t[:, h, :], Qd.t[:, h, rs], True, False, [Sb.r, Qd.r], [bK.r])
                        mm(o, Vn.t[:, h, :], attnT.t[:, h, rs], False, True, [Vn.r, attnT.r], [bK.r])
                bL = nb()
                for h in range(4):
                    mm(bL.t[:, h * 128:(h + 1) * 128], Kd.t[:, h, :], Vn.t[:, h, :], True, True, [Kd.r, Vn.r], [bL.r])
                for h in range(4):
                    col = s * 16 + ci * 4 + h
                    stt(S.t[:, h, :], S.t[:, h, :], eGl.t[:, col:col + 1], bL.t[:, h * 128:(h + 1) * 128], ALU.mult, ALU.add,
                        [S.rh[h], eGl.r, bL.r], [S.rh[h]])
                act(Sb.t[:], S.t[:], AF.Copy, [S.r], [Sb.r])
                if s == 0:
                    yield
            if full:
                act(Ot.t[:, :, cs], h4(bK.t[:]), AF.Copy, [bK.r], [Ot.r])
            yield

        def stage_mid(full):
            for ci in range(4):
                gdn_front(ci, full)
            for lvl in range(7):
                for ci in range(4):
                    gdn_level(ci, lvl)

        def gen_states(full, eGl):
            for ci in range(4):
                yield from gen_gdn_state(ci, full, eGl)

        def merge(ga, gb, ra, rb):
            da = db = False
            while not (da and db):
                for _ in range(ra):
                    if not da:
                        try:
                            next(ga)
                        except StopIteration:
                            da = True
                for _ in range(rb):
                    if not db:
                        try:
                            next(gb)
                        except StopIteration:
                            db = True

        def gen_empty():
            return
            yield

        def stage_gdn_out():
            bks = []
            for h in range(4):
                sq = rot(sqb, "s")
                act(sq.t[:], Ot.t[:, h, :], AF.Square, [Ot.r], [sq.r])
                bk = nb()
                mm(bk.t[:], ones.t[:], sq.t[:], True, True, [ones.r, sq.r], [bk.r])
                bks.append(bk)
            for h in range(4):
                act(raw[h].t[:], bks[h].t[:], AF.Ln, [bks[h].r], [raw[h].r], bias=RMS_EPS, scale=1.0 / 128.0)
            for h in range(4):
                act(raw[h].t[:], raw[h].t[:], AF.Exp, [raw[h].r], [raw[h].r], scale=-0.5)
            for h in range(4):
                stt(raw[h].t[:], raw[h].t[:], vecs.t[:, V_GNW:V_GNW + 1], zs.t[:, h, :], ALU.mult, ALU.mult, [raw[h].r, vecs.r, zs.r], [raw[h].r])
                tt(yT.t[:, h, :], Ot.t[:, h, :], raw[h].t[:], ALU.mult, [Ot.r, raw[h].r], [yT.r])

        def stage_out(xb):
            for pc in range(2):
                for cc in range(4):
                    dc = pc * 4 + cc
                    if cc % 2 == 0:
                        slot = piece_rows("wout", pc * 128, 8, 512, (cc // 2) * 256, 256)
                        wv = slot.wv
                    bk = proj_chunk(wv, cc % 2, slot, yT, 8)
                    if cc % 2 == 1:
                        slot.pinned = False
                    stt(xb.t[:, dc, :], bk.t[:], gate1[:, dc:dc + 1], xb.t[:, dc, :], ALU.mult, ALU.add, [bk.r, mod.r, xb.r], [xb.r])

        def gen_ffn(xb):
            for i in range(6):
                w = 512 if i < 5 else 256
                for hf_ in range(w // 256):
                    sg_ = piece_rows("wg", i * 128, 8, w, hf_ * 256, 256)
                    su_ = piece_rows("wu", i * 128, 8, w, hf_ * 256, 256)
                    for c2 in range(2):
                        fc = i * 4 + hf_ * 2 + c2
                        bg = proj_chunk(sg_.wv, c2, sg_, h2T, 8)
                        bu = proj_chunk(su_.wv, c2, su_, h2T, 8)
                        if c2 == 1:
                            sg_.pinned = False
                            su_.pinned = False
                        s_ = rot(sgb, "sg")
                        act(s_.t[:], bg.t[:], AF.Silu, [bg.r], [s_.r])
                        tt(actv[fc].t, bu.t[:], s_.t[:], ALU.mult, [bu.r, s_.r], [actv[fc].r])
                        yield
            for dc in range(8):
                slA = piece(D["wd"][dc * 128:(dc + 1) * 128, 0:11 * 128].rearrange("p (j n) -> p j n", j=11), 11, 128)
                slB = piece(D["wd"][dc * 128:(dc + 1) * 128, 11 * 128:22 * 128].rearrange("p (j n) -> p j n", j=11), 11, 128)
                bk = nb()
                for fc in range(22):
                    sl = slA if fc < 11 else slB
                    mm(bk.t[:], sl.wv[:, fc % 11, :], actv[fc].t, fc == 0, fc == 21, [sl.r, actv[fc].r], [bk.r])
                slA.pinned = False
                slB.pinned = False
                stt(xb.t[:, dc, :], bk.t[:], gate2[:, dc:dc + 1], xb.t[:, dc, :], ALU.mult, ALU.add, [bk.r, mod.r, xb.r], [xb.r])
                yield

        def stage_final(xb, ti):
            rms_rstd([(xb.t[:, j, :], xb.r) for j in range(8)], 8, 1.0 / 1024.0, RMS_EPS)
            for j in range(8):
                stt(xb.t[:, j, :], xb.t[:, j, :], vecs.t[:, V_NF + j:V_NF + j + 1], rstd.t[:], ALU.mult, ALU.mult,
                    [xb.r, vecs.r, rstd.r], [xb.r])
            dstv = outd.rearrange("(j p) t -> p j t", p=128)[:, :, ti * TT:(ti + 1) * TT]
            fw.dma(SP, out_ds, lambda e: e.dma_start(out=dstv, in_=xb.t[:]), [xb.r], [])
            return None

        seq = [("xp", ti, False) for ti in range(NT)] + [("xm", ti, True) for ti in range(NT)]
        if DBG["seq"] is not None:
            seq = DBG["seq"]
        dbg("mod", mod.t[:], mod.r)
        def gen_pre(n):
            src, ti, is_main = seq[n]
            xb = xs[n % 2]
            full = is_main or (ti == NT - 1)
            yield from gen_norm(xb, a1, shift1, hT)
            cur["n"] = n
            dbg("hT", hT.t[:], hT.r)
            yield from gen_proj_qkv(full)
            cur["n"] = n
            dbg("kn", kn.t[:], kn.r); dbg("vT", vT.t[:], vT.r); dbg("qn", qn.t[:], qn.r)
            stage_gates(absb, not is_main, eGlS[n % 2])
            for nm in ["g", "beta", "cP", "skbg", "skd", "cA"]:
                dbg("gt_" + nm, gt[nm].t[:], gt[nm].r)
            dbg("gsb", gsb.t[:], gsb.r)
            yield

        def gen_halo_mask():
            ts(halo.t[:], halo.t[:], vecs.t[:, V_PM:V_PM + 1], ALU.mult, [halo.r, vecs.r], [halo.r])
            yield

        def chain(*gens):
            for g_ in gens:
                yield from g_

        load_x(D[seq[0][0]], seq[0][1], 0)
        for _ in gen_pre(0):
            pass
        for n, (src, ti, is_main) in enumerate(seq):
            xb = xs[n % 2]
            cur["n"] = n
            has_next = n + 1 < len(seq)
            if has_next:
                load_x(D[seq[n + 1][0]], seq[n + 1][1], (n + 1) % 2)
            full = is_main or (ti == NT - 1)
            stage_mid(is_main)
            nxt = gen_pre(n + 1) if has_next else gen_empty()
            if is_main:
                merge(gen_states(True, eGlS[n % 2]), gen_proj_rest(), 1, 2)
                cur["n"] = n
                dbg("zs", zs.t[:], zs.r); dbg("ysc", yT.t[:], yT.r)
                dbg("S", S.t[:], S.r)
                dbg("Ot", Ot.t[:], Ot.r)
                stage_gdn_out()
                dbg("yT", yT.t[:], yT.r)
                stage_out(xb)
                dbg("x1", xb.t[:], xb.r)
                for _ in gen_norm(xb, a2, shift2, h2T):
                    pass
                merge(gen_ffn(xb), nxt, 1, 1)
                cur["n"] = n
                dbg("x2", xb.t[:], xb.r)
                stage_final(xb, ti)
            else:
                if full:
                    others = chain(gen_proj_rest(), gen_halo_mask(), nxt)
                else:
                    others = nxt
                merge(gen_states(False, eGlS[n % 2]), others, 1, 3)
        SP.prog.append(("wait", out_ds.sem, out_ds.count))
        if dbg_ds.count:
            SP.prog.append(("wait", dbg_ds.sem, dbg_ds.count))
        fw.finish()
    return nc


def _piece(W, c0, c1, nk=8):
    w = np.ascontiguousarray(W[:, c0:c1])
    n = c1 - c0
    return np.ascontiguousarray(w.reshape(nk, 128, n).transpose(1, 0, 2)).reshape(128, nk * n)


def _pad(a, n):
    out = np.zeros((a.shape[0], n), np.float32)
    out[:, :a.shape[1]] = a
    return out


def _fm(v):
    return np.ascontiguousarray(np.asarray(v, np.float32).reshape(-1, 128).T)


_CACHE = {}


def kernel(x, c, w_ada, b_ada, norm1_w, w_in, conv_qkv_w, A_log, dt_bias, gdn_norm_w,
           conv_short_w, w_out, norm2_w, w_gate, w_up, w_down, norm_f_w):
    f = lambda a: np.asarray(a, np.float32)
    x, c = f(x), f(c)
    w_ada, b_ada, w_in, w_out = f(w_ada)[0], f(b_ada)[0], f(w_in)[0], f(w_out)[0]
    w_gate, w_up, w_down = f(w_gate)[0], f(w_up)[0], f(w_down)[0]
    idx = np.arange(128)
    same = (idx[:, None] // 64) == (idx[None, :] // 64)
    idf = np.eye(128, dtype=np.float32)
    tri = (same & (idx[:, None] <= idx[None, :])).astype(np.float32)
    blk = same.astype(np.float32)
    indA = np.repeat((idx < 64).astype(np.float32)[:, None], 128, 1)
    indB = np.repeat((idx >= 64).astype(np.float32)[:, None], 128, 1)
    mnegs = np.where(same & (idx[:, None] > idx[None, :]), 0.0, NEG).astype(np.float32)
    mnegi = np.where(same & (idx[None, :] >= idx[:, None]), 0.0, NEG).astype(np.float32)
    consts = np.concatenate([idf, tri, blk, indA, indB, mnegs, mnegi], 1)
    wada = np.concatenate([_piece(w_ada, p * 512, (p + 1) * 512) for p in range(12)], 0)
    cols = [(0, 512), (512, 1024), (1024, 1536), (1536, 2048), (2568, 3080), (3080, 3592), (2056, 2568)]
    win = np.concatenate([_piece(w_in, a, b) for a, b in cols], 0)
    wab = _piece(w_in, 2048, 2056)
    wout = np.concatenate([_piece(w_out, p * 512, (p + 1) * 512) for p in range(2)], 0)
    gcols = [(i * 512, min((i + 1) * 512, 2816)) for i in range(6)]
    wg = np.concatenate([_pad(_piece(w_gate, a, b), 4096) for a, b in gcols], 0)
    wu = np.concatenate([_pad(_piece(w_up, a, b), 4096) for a, b in gcols], 0)
    wd = np.concatenate([_piece(w_down, d * 128, (d + 1) * 128, nk=22) for d in range(8)], 0)
    cw = f(conv_qkv_w)[0]
    cs = f(conv_short_w)[0]
    cwT = np.ascontiguousarray(cw.reshape(4, 12, 128).transpose(2, 1, 0)).reshape(128, 48)
    csT = np.ascontiguousarray(cs.reshape(3, 4, 128).transpose(2, 1, 0)).reshape(128, 12)
    in_maps = []
    for r in range(8):
        b, hf = r // 2, r % 2
        vecs = np.zeros((128, NV), np.float32)
        vecs[:, V_C:V_C + 8] = _fm(c[b])
        vecs[:, V_BADA:V_BADA + 48] = _fm(b_ada)
        vecs[:, V_N1:V_N1 + 8] = _fm(f(norm1_w)[0])
        vecs[:, V_N2:V_N2 + 8] = _fm(f(norm2_w)[0])
        vecs[:, V_NF:V_NF + 8] = _fm(f(norm_f_w))
        vecs[:, V_CW:V_CW + 48] = cwT
        vecs[:, V_CS:V_CS + 12] = csT
        vecs[:, V_ALOG:V_ALOG + 4] = f(A_log)[0][None, :]
        vecs[:, V_DTB:V_DTB + 4] = f(dt_bias)[0][None, :]
        vecs[:, V_GNW] = f(gdn_norm_w)[0]
        vecs[:, V_PM] = float(hf)
        in_maps.append({
            "xm": np.ascontiguousarray(x[b, hf * 4096:(hf + 1) * 4096].T),
            "xp": np.ascontiguousarray(x[b, 0:4096].T),
            "consts": consts, "vecs": vecs, "wada": wada, "win": win, "wab": wab, "wout": wout,
            "wg": wg, "wu": wu, "wd": wd,
        })
    if "nc" not in _CACHE:
        _CACHE["nc"] = build_program()
    if DBG["on"]:
        res = run_bass_kernel_spmd(_CACHE["nc"], in_maps[:DBG.get("ncores", 8)], core_ids=list(range(DBG.get("ncores", 8))))
        DBG["res"] = res.results
        return None
    res = run_bass_kernel_spmd(_CACHE["nc"], in_maps, core_ids=list(range(8)))
    out = np.empty((4, 8192, 1024), np.float32)
    for r in range(8):
        b, hf = r // 2, r % 2
        out[b, hf * 4096:(hf + 1) * 4096] = res.results[r]["out"].T
    return out
```
